# Optimizing a Trainium2 kernel written in Bass

```python
import jax, jax.numpy as jnp
from jax import lax
import numpy as np

D_MODEL = 2048
BATCH = 2
SEQ = 16384
DEPTH = 2

D_FF = 5632
N_MIXERS = 2
N_GLA = (DEPTH + 1) // 2
N_POOL = DEPTH // 2
GLA_HEADS = 4
GLA_HEAD_K = D_MODEL // (2 * GLA_HEADS)
GLA_HEAD_V = D_MODEL // GLA_HEADS
GLA_K = GLA_HEADS * GLA_HEAD_K
GLA_V = GLA_HEADS * GLA_HEAD_V
GLA_RANK = 16
GLA_TAU = 16.0
GLA_CHUNK = 64
GLA_SPLITS = (GLA_K, 2 * GLA_K, 2 * GLA_K + GLA_V, 2 * GLA_K + 2 * GLA_V,
              2 * GLA_K + 2 * GLA_V + GLA_RANK)
GLA_IN = 2 * GLA_K + 2 * GLA_V + 2 * GLA_RANK
POOL_WINDOWS = (2, 4, 8, 16)
POOL_GROUPS = len(POOL_WINDOWS)
POOL_G = D_MODEL // POOL_GROUPS
NORM_EPS = 1e-6

kernel_name = "hybrid_gla_pool_macaron_encoder"


def _rmsnorm(x, g):
    x32 = x.astype(jnp.float32)
    y = x32 * lax.rsqrt(jnp.mean(x32 * x32, axis=-1, keepdims=True) + NORM_EPS)
    return (y * g.astype(jnp.float32)).astype(x.dtype)


def _swiglu(x, w_gate, w_up, w_down):
    return (jax.nn.silu(x @ w_gate) * (x @ w_up)) @ w_down


def _gla_one_direction(q, k, v, log_a):
    B, H, L, dk = q.shape
    dv = v.shape[-1]
    n = L // GLA_CHUNK
    C = GLA_CHUNK
    q = q.reshape(B, H, n, C, dk)
    k = k.reshape(B, H, n, C, dk)
    v = v.reshape(B, H, n, C, dv)
    b = jnp.cumsum(log_a.reshape(B, H, n, C, dk), axis=3)
    b_last = b[:, :, :, -1:, :]
    q_in = q * jnp.exp(b)
    k_in = k * jnp.exp(-b)
    k_end = k * jnp.exp(b_last - b)
    mask = jnp.tril(jnp.ones((C, C), dtype=bool))
    scores = jnp.where(mask, jnp.einsum('bhncd,bhnsd->bhncs', q_in, k_in), 0.0)
    o_intra = jnp.einsum('bhncs,bhnse->bhnce', scores, v)

    def step(S, inp):
        qc, kc, vc, dc = inp
        o = jnp.einsum('bhcd,bhde->bhce', qc, S)
        S = S * dc[..., None] + jnp.einsum('bhcd,bhce->bhde', kc, vc)
        return S, o

    xs = (jnp.moveaxis(q_in, 2, 0), jnp.moveaxis(k_end, 2, 0), jnp.moveaxis(v, 2, 0),
          jnp.moveaxis(jnp.exp(b_last[:, :, :, 0, :]), 2, 0))
    S0 = jnp.zeros((B, H, dk, dv), jnp.float32)
    _, o_inter = lax.scan(step, S0, xs)
    o = o_intra + jnp.moveaxis(o_inter, 0, 2)
    return o.reshape(B, H, L, dv)


def _gla_mixer(h, w_in, w_gate_up, b_gate, head_gain, w_out):
    B, L, _ = h.shape
    proj = h @ w_in
    q, k, v, r, z_f, z_b = jnp.split(proj, GLA_SPLITS, axis=-1)

    def heads(t, d):
        return t.reshape(B, L, GLA_HEADS, d).transpose(0, 2, 1, 3).astype(jnp.float32)

    log_a_f = jax.nn.log_sigmoid(z_f.astype(jnp.float32) @ w_gate_up[0].astype(jnp.float32)
                                 + b_gate[0].astype(jnp.float32)) / GLA_TAU
    log_a_b = jax.nn.log_sigmoid(z_b.astype(jnp.float32) @ w_gate_up[1].astype(jnp.float32)
                                 + b_gate[1].astype(jnp.float32)) / GLA_TAU
    qh = heads(q, GLA_HEAD_K) * (GLA_HEAD_K ** -0.5)
    kh = heads(k, GLA_HEAD_K)
    vh = heads(v, GLA_HEAD_V)
    flip = lambda t: jnp.flip(t, axis=2)
    o_f = _gla_one_direction(qh, kh, vh, heads(log_a_f, GLA_HEAD_K))
    o_b = flip(_gla_one_direction(flip(qh), flip(kh), flip(vh), flip(heads(log_a_b, GLA_HEAD_K))))
    o = o_f + o_b
    o = o * lax.rsqrt(jnp.mean(o * o, axis=-1, keepdims=True) + NORM_EPS) * head_gain.astype(jnp.float32)
    o = o.transpose(0, 2, 1, 3).reshape(B, L, GLA_V).astype(h.dtype)
    return (o * jax.nn.silu(r)) @ w_out


def _pool_mixer(h, w_pool, b_pool, scale):
    B, L, D = h.shape
    h32 = h.astype(jnp.float32)
    cs = jnp.concatenate([jnp.zeros((B, 1, D), jnp.float32), jnp.cumsum(h32, axis=1)], axis=1)
    t = jnp.arange(L)
    means = []
    for g, w in enumerate(POOL_WINDOWS):
        lo = jnp.clip(t - w // 2, 0, L)
        hi = jnp.clip(t + w // 2, 0, L)
        cg = cs[:, :, g * POOL_G:(g + 1) * POOL_G]
        s = jnp.take(cg, hi, axis=1) - jnp.take(cg, lo, axis=1)
        cnt = (hi - lo).astype(jnp.float32)[None, :, None]
        means.append(s / cnt)
    pooled = jnp.stack(means, axis=2) - h32.reshape(B, L, POOL_GROUPS, POOL_G)
    y = jnp.einsum('blgc,gcd->blgd', pooled.astype(h.dtype), w_pool).reshape(B, L, D)
    return (y + b_pool) * scale


def setup_inputs(seed: int = 0) -> dict:
    key = jax.random.key(seed)
    ks = jax.random.split(key, 16)
    nrm = jax.random.normal
    f32 = jnp.float32
    return {
        "x": nrm(ks[0], (BATCH, SEQ, D_MODEL), f32),
        "ffn_w_gate": nrm(ks[1], (DEPTH, 2, D_MODEL, D_FF), f32) * D_MODEL ** -0.5,
        "ffn_w_up": nrm(ks[2], (DEPTH, 2, D_MODEL, D_FF), f32) * D_MODEL ** -0.5,
        "ffn_w_down": nrm(ks[3], (DEPTH, 2, D_FF, D_MODEL), f32) * D_FF ** -0.5,
        "norm_gain": 1.0 + 0.05 * nrm(ks[4], (DEPTH, 6, D_MODEL), f32),
        "gla_w_in": nrm(ks[5], (N_GLA, D_MODEL, GLA_IN), f32) * D_MODEL ** -0.5,
        "gla_w_gate_up": nrm(ks[6], (N_GLA, 2, GLA_RANK, GLA_K), f32) * GLA_RANK ** -0.5,
        "gla_b_gate": 0.1 * nrm(ks[7], (N_GLA, 2, GLA_K), f32),
        "gla_head_gain": 1.0 + 0.05 * nrm(ks[8], (N_GLA, GLA_HEAD_V), f32),
        "gla_w_out": nrm(ks[9], (N_GLA, GLA_V, D_MODEL), f32) * GLA_V ** -0.5,
        "pool_w": nrm(ks[10], (N_POOL, POOL_GROUPS, POOL_G, POOL_G), f32) * POOL_G ** -0.5,
        "pool_b": 0.01 * nrm(ks[11], (N_POOL, D_MODEL), f32),
        "pool_scale": 1.0 + 0.1 * nrm(ks[12], (N_POOL, D_MODEL), f32),
    }


def reference(x, ffn_w_gate, ffn_w_up, ffn_w_down, norm_gain, gla_w_in, gla_w_gate_up,
              gla_b_gate, gla_head_gain, gla_w_out, pool_w, pool_b, pool_scale):
    for i in range(DEPTH):
        g = norm_gain[i]
        x = x + 0.5 * _rmsnorm(_swiglu(_rmsnorm(x, g[0]), ffn_w_gate[i, 0], ffn_w_up[i, 0],
                                       ffn_w_down[i, 0]), g[1])
        h = _rmsnorm(x, g[2])
        j = i // N_MIXERS
        if i % N_MIXERS == 0:
            m = _gla_mixer(h, gla_w_in[j], gla_w_gate_up[j], gla_b_gate[j], gla_head_gain[j], gla_w_out[j])
        else:
            m = _pool_mixer(h, pool_w[j], pool_b[j], pool_scale[j])
        x = x + _rmsnorm(m, g[3])
        x = x + 0.5 * _rmsnorm(_swiglu(_rmsnorm(x, g[4]), ffn_w_gate[i, 1], ffn_w_up[i, 1],
                                       ffn_w_down[i, 1]), g[5])
    return x
```

```python
import os
from contextlib import ExitStack

import numpy as np
import concourse.bass as bass
import concourse.mybir as mybir
from concourse.bass_utils import run_bass_kernel_spmd

F32 = mybir.dt.float32
BF16 = mybir.dt.bfloat16
AF = mybir.ActivationFunctionType
ALU = mybir.AluOpType

NCORES = 8
D = 2048
KC = 16
FF = 5632
FC = 44
NT = 4096
T = 512
NTILE = NT // T
CH = 128
NCH = NT // CH
LS = 16384
NTS = LS // T
NCS = LS // CH
NEXT = NTILE + 2
EPS = 1e-6
SEM_MAX = 30000

ENGS = ["sync", "act", "dve", "pe", "pool"]
BLK = {"sync": "sync", "act": "scalar", "dve": "vector", "pe": "tensor", "pool": "gpsimd"}


class _Op:
    __slots__ = ("eng", "fn", "dma", "deps", "flag", "sig")


class _Chan:
    def __init__(self, prog, name):
        self.prog = prog
        self.name = name
        self.sem = None
        self.val = 0
        self.gen = 0

    def signal(self, inc):
        if self.sem is None or self.val + inc > SEM_MAX:
            self.sem = self.prog.new_sem("%s_%d" % (self.name, self.gen))
            self.gen += 1
            self.val = 0
        self.val += inc
        return (self.sem, self.val, inc)


class Prog:
    def __init__(self, nc, stack):
        self.nc = nc
        self.stack = stack
        self.nsem = 0
        self.chan = {}
        self.waited = {e: {} for e in ENGS}
        self.reset()

    def new_sem(self, name):
        self.nsem += 1
        return self.stack.enter_context(self.nc.semaphore("s%d_%s" % (self.nsem, name)))

    def reset(self):
        self.ops = {e: [] for e in ENGS}
        self.lastw = {}
        self.readers = {}
        self.dma_last = {}

    def _chan(self, key):
        c = self.chan.get(key)
        if c is None:
            c = self.chan[key] = _Chan(self, str(key).replace(" ", "").replace("'", "").replace("(", "").replace(")", "").replace(",", "_"))
        return c

    def add(self, eng, fn, reads=(), writes=(), dma=None):
        op = _Op()
        op.eng = eng
        op.fn = fn
        op.dma = dma
        op.flag = False
        op.sig = None
        deps = []
        for r in reads:
            w = self.lastw.get(r)
            if w is not None:
                deps.append(w)
        for r in writes:
            w = self.lastw.get(r)
            if w is not None:
                deps.append(w)
            deps.extend(self.readers.get(r, ()))
        dd = []
        seen = set()
        for d in deps:
            if id(d) in seen or d is op:
                continue
            seen.add(id(d))
            if d.dma is None and d.eng == eng:
                continue
            d.flag = True
            dd.append(d)
        op.deps = dd
        if dma is not None:
            op.sig = self._chan(("dma", dma)).signal(16)
            self.dma_last[dma] = op
        for r in writes:
            self.lastw[r] = op
            self.readers[r] = []
        for r in reads:
            lst = self.readers.setdefault(r, [])
            if op.dma is None:
                lst[:] = [x for x in lst if not (x.dma is None and x.eng == eng)]
            lst.append(op)
        self.ops[eng].append(op)
        return op

    def emit(self, name=None):
        nc = self.nc
        finals = []
        for e in ENGS:
            lst = [o for o in self.ops[e] if o.dma is None]
            if lst:
                lst[-1].flag = True
            for o in self.ops[e]:
                if o.dma is None and o.flag:
                    o.sig = self._chan(("eng", e)).signal(1)
            if lst:
                finals.append(lst[-1].sig)
        for k, o in self.dma_last.items():
            finals.append(o.sig)
        prog = self

        def body_for(e):
            def body(eng):
                wd = prog.waited[e]

                def wait(sig):
                    sem, val, _ = sig
                    k = id(sem)
                    if wd.get(k, 0) >= val:
                        return
                    wd[k] = val
                    eng.wait_ge(sem, val)

                for op in prog.ops[e]:
                    for d in op.deps:
                        wait(d.sig)
                    ins = op.fn(eng)
                    if op.sig is not None:
                        ins.then_inc(op.sig[0], op.sig[2])
                for f in finals:
                    wait(f)
            return body

        with nc.Block() as block:
            for e in ENGS:
                getattr(block, BLK[e])(body_for(e))
        self.reset()

    def mm(self, out, lhsT, rhs, start, stop, reads, writes):
        return self.add("pe", lambda e: e.matmul(out, lhsT=lhsT, rhs=rhs, start=start, stop=stop), reads, writes)

    def tr(self, out, in_, ident, reads, writes):
        return self.add("pe", lambda e: e.transpose(out=out, in_=in_, identity=ident), reads, writes)

    def act(self, out, in_, func, reads, writes, **kw):
        return self.add("act", lambda e: e.activation(out=out, in_=in_, func=func, **kw), reads, writes)

    def stt(self, out, in0, scalar, in1, op0, op1, reads, writes):
        return self.add("dve", lambda e: e.scalar_tensor_tensor(out=out, in0=in0, scalar=scalar, in1=in1, op0=op0, op1=op1), reads, writes)

    def tt(self, out, in0, in1, op, reads, writes, eng="dve"):
        return self.add(eng, lambda e: e.tensor_tensor(out=out, in0=in0, in1=in1, op=op), reads, writes)

    def ts(self, out, in0, s1, s2, op0, op1, reads, writes):
        if s2 is None or op1 is None:
            return self.add("dve", lambda e: e.tensor_scalar(out=out, in0=in0, scalar1=s1, scalar2=None, op0=op0), reads, writes)
        return self.add("dve", lambda e: e.tensor_scalar(out=out, in0=in0, scalar1=s1, scalar2=s2, op0=op0, op1=op1), reads, writes)

    def recip(self, out, in_, reads, writes):
        return self.add("dve", lambda e: e.reciprocal(out=out, in_=in_), reads, writes)

    def copy(self, out, in_, reads, writes, eng="dve"):
        return self.add(eng, lambda e: e.tensor_copy(out=out, in_=in_), reads, writes)

    def memset(self, out, val, writes, eng="dve"):
        return self.add(eng, lambda e: e.memset(out, val), (), writes)

    def load(self, out, in_, key, reads, writes, **kw):
        return self.add("sync", lambda e: e.dma_start(out=out, in_=in_, **kw), reads, writes, dma=key)

    def store(self, out, in_, key, reads, writes, **kw):
        return self.add("pool", lambda e: e.dma_start(out=out, in_=in_, **kw), reads, writes, dma=key)


class Pack:
    def __init__(self):
        self.parts = []
        self.offs = {}
        self.n = 0

    def add(self, name, arr):
        arr = np.ascontiguousarray(arr, dtype=np.float32).reshape(128, -1)
        self.offs[name] = self.n
        self.parts.append(arr)
        self.n += arr.shape[1]

    def build(self):
        pad = (-self.n) % 4096
        if pad:
            self.parts.append(np.zeros((128, pad), np.float32))
            self.n += pad
        return np.ascontiguousarray(np.concatenate(self.parts, axis=1))


def _lhs_layout(w, nj):
    K, N = w.shape
    kk = K // 128
    a = w.reshape(kk, 128, nj, N // nj)
    return np.ascontiguousarray(a.transpose(1, 2, 0, 3)).reshape(128, -1)


def pack_ffn(pk, tag, wg, wu, wd):
    pk.add(tag + "g", _lhs_layout(wg, FC))
    pk.add(tag + "u", _lhs_layout(wu, FC))
    pk.add(tag + "d", _lhs_layout(wd, KC))


def pack_gla_in(pk, w_in, w_gate_up, b_gate):
    qkr = np.concatenate([w_in[:, 0:2048], w_in[:, 4096:6144]], axis=1)
    pk.add("gqkr", _lhs_layout(qkr, 32))
    wv = w_in[:, 2048:4096]
    pk.add("gv", _lhs_layout(wv, 8))
    pk.add("gz", _lhs_layout(w_in[:, 6144:6176], 1))
    wup = np.zeros((128, 2, 1024), np.float32)
    wup[0:16] = np.transpose(w_gate_up, (1, 0, 2))
    wup[16] = b_gate
    pk.add("gwup", wup)


class Ctx:
    pass


_UNIQ = [0]


def _u(name):
    _UNIQ[0] += 1
    return "%s_%d" % (name, _UNIQ[0])


def alloc_common(nc, st, C):
    C.ones = st.enter_context(nc.sbuf_tensor(_u("ones"), [128, 128], BF16))
    C.epsT = st.enter_context(nc.sbuf_tensor(_u("epsT"), [128, 1], F32))
    C.eps4T = st.enter_context(nc.sbuf_tensor(_u("eps4T"), [128, 1], F32))
    C.gc = st.enter_context(nc.sbuf_tensor(_u("gc"), [128, 192], F32))


def init_common(P, C, gcols):
    P.memset(C.ones[:], 1.0, ["ones"])
    P.memset(C.epsT[:], EPS, ["epsT"])
    P.memset(C.eps4T[:], 4.0 * EPS, ["epsT"])
    P.load(C.gc[:], gcols, "gc", [], ["gc"])


def alloc_ffn_bufs(nc, st, C):
    C.xs = st.enter_context(nc.sbuf_tensor(_u("xs"), [128, KC, T], F32))
    C.U = st.enter_context(nc.sbuf_tensor(_u("U"), [128, KC * (T + 16)], F32))
    C.y = C.U[:, 0:KC * T].rearrange("p (k t) -> p k t", k=KC)
    C.xn = C.U[:, 0:KC * T // 2].bitcast(BF16).rearrange("p (k t) -> p k t", k=KC)
    C.vst = C.U[:, KC * T // 2:KC * T].bitcast(BF16).rearrange("p (c f) -> p c f", c=4)
    C.hp = C.U[:, :].rearrange("p (k t) -> p k t", k=KC)
    C.hT = st.enter_context(nc.sbuf_tensor(_u("hT"), [128, FC, T], BF16))
    C.sq = C.hT
    C.rstd = st.enter_context(nc.sbuf_tensor(_u("rstd"), [128, T], F32))
    C.tmp = st.enter_context(nc.sbuf_tensor(_u("tmp"), [128, T], F32))
    C.tmpc = st.enter_context(nc.sbuf_tensor(_u("tmpc"), [128, 2, T], F32))
    C.sg = C.tmpc
    C.sqc = st.enter_context(nc.sbuf_tensor(_u("sqc"), [128, 2, T], BF16))
    C.NW = 2
    C.wg = [st.enter_context(nc.sbuf_tensor(_u("wg%d" % i), [128, KC, 128], BF16)) for i in range(C.NW)]
    C.wu = [st.enter_context(nc.sbuf_tensor(_u("wu%d" % i), [128, KC, 128], BF16)) for i in range(C.NW)]
    C.wd = [st.enter_context(nc.sbuf_tensor(_u("wd%d" % i), [128, FC, 128], BF16)) for i in range(2)]
    C.psA = [st.enter_context(nc.psum_tensor(_u("psA%d" % i), [128, T], F32)) for i in range(2)]
    C.psB = [st.enter_context(nc.psum_tensor(_u("psB%d" % i), [128, T], F32)) for i in range(2)]
    C.psC = [st.enter_context(nc.psum_tensor(_u("psC%d" % i), [128, T], F32)) for i in range(2)]
    C.pss = st.enter_context(nc.psum_tensor(_u("pss"), [128, T], F32))


XN = ["U0"]
YY = ["U0", "U1"]


def hreg(j):
    if j < 16:
        return "h0"
    if j < 32:
        return "h1"
    return ("h2", (j - 32) // 4)


HR = ["h1", ("h2", 0), ("h2", 1), ("h2", 2)]


def rstd_from_pss(P, C, dn, half=False):
    f2 = 4.0 if half else 1.0
    P.act(C.tmp[:], C.pss[:], AF.Sqrt, ["pss", "epsT"], ["tmp"], scale=f2 / dn, bias=(C.eps4T[:] if half else C.epsT[:]))
    P.recip(C.rstd[:], C.tmp[:], ["tmp"], ["rstd"])


def ssq_xs(P, C):
    P.act(C.sq[:, 0:KC, :], C.xs[:], AF.Square, ["xs"], ["h0"])
    for k in range(KC):
        P.mm(C.pss[:], C.ones[:], C.sq[:, k, :], k == 0, k == KC - 1, ["h0", "ones"], ["pss"])


def prenorm(P, C, gcol0):
    ssq_xs(P, C)
    rstd_from_pss(P, C, D)
    for k in range(KC):
        P.stt(C.xn[:, k, :], C.xs[:, k, :], C.gc[:, gcol0 + k:gcol0 + k + 1], C.rstd[:], ALU.mult, ALU.mult,
              ["xs", "gc", "rstd"], XN)


def postnorm_residual(P, C, half):
    rstd_from_pss(P, C, D, half)
    for m in range(KC):
        b = m % 2
        P.tt(C.tmpc[:, b, :], C.y[:, m, :], C.rstd[:], ALU.mult, YY + ["rstd"], [("tmpc", b)])
        P.tt(C.xs[:, m, :], C.xs[:, m, :], C.tmpc[:, b, :], ALU.add, ["xs", ("tmpc", b)], ["xs"])


def evac_y_chunk(P, C, ps, psname, m, gcol0, pending):
    b = m % 2
    P.act(C.y[:, m, :], ps, AF.Copy, [psname, "gc"], YY, scale=C.gc[:, gcol0 + m:gcol0 + m + 1])
    P.act(C.sqc[:, b, :], ps, AF.Square, [psname], [("sqc", b)])
    if pending is not None:
        pm = pending
        P.mm(C.pss[:], C.ones[:], C.sqc[:, pm % 2, :], pm == 0, False, [("sqc", pm % 2), "ones"], ["pss"])
    return m


def finish_ssq(P, C, pending):
    pm = pending
    P.mm(C.pss[:], C.ones[:], C.sqc[:, pm % 2, :], pm == 0, True, [("sqc", pm % 2), "ones"], ["pss"])


def ffn(P, C, wallb, og, ou, od, gpre, gpost):
    NW = C.NW

    def load_gu(j):
        s = j % NW
        P.load(C.wg[s][:], wallb[:, og + j * 2048:og + (j + 1) * 2048].rearrange("p (k c) -> p k c", k=KC),
               ("wg", s), [], [("wg", s)])
        P.load(C.wu[s][:], wallb[:, ou + j * 2048:ou + (j + 1) * 2048].rearrange("p (k c) -> p k c", k=KC),
               ("wu", s), [], [("wu", s)])

    def load_d(m):
        s = m % 2
        P.load(C.wd[s][:], wallb[:, od + m * 5632:od + (m + 1) * 5632].rearrange("p (j c) -> p j c", j=FC),
               ("wd", s), [], [("wd", s)])

    for j in range(NW - 1):
        load_gu(j)
    prenorm(P, C, gpre)
    for j in range(FC):
        if j + NW - 1 < FC:
            load_gu(j + NW - 1)
        if j == 20:
            load_d(0)
        if j == 32:
            load_d(1)
        s = j % NW
        b = j % 2
        for k in range(KC):
            P.mm(C.psA[b][:], C.wg[s][:, k, :], C.xn[:, k, :], k == 0, k == KC - 1, [("wg", s)] + XN, [("psA", b)])
        for k in range(KC):
            P.mm(C.psB[b][:], C.wu[s][:, k, :], C.xn[:, k, :], k == 0, k == KC - 1, [("wu", s)] + XN, [("psB", b)])
        P.act(C.sg[:, b, :], C.psA[b][:], AF.Silu, [("psA", b)], [("tmpc", b)])
        P.tt(C.hT[:, j, :], C.psB[b][:], C.sg[:, b, :], ALU.mult, [("psB", b), ("tmpc", b)], [hreg(j)])
    pending = None
    for m in range(KC):
        s = m % 2
        b = m % 2
        for j in range(FC):
            P.mm(C.psC[b][:], C.wd[s][:, j, :], C.hT[:, j, :], j == 0, j == FC - 1, [("wd", s), hreg(j)], [("psC", b)])
        if m + 2 < KC:
            load_d(m + 2)
        pending = evac_y_chunk(P, C, C.psC[b][:], ("psC", b), m, gpost, pending)
    finish_ssq(P, C, pending)
    postnorm_residual(P, C, True)


def cast_cols(P, wall, wallb, c0, c1):
    CW = 4096
    for i in range(c0 // CW, c1 // CW):
        P.add("pool", lambda e, i=i: e.dma_start(out=wallb[:, i * CW:(i + 1) * CW], in_=wall[:, i * CW:(i + 1) * CW],
                                                   max_dma_last_dim=8192),
              [], [("wallb", i)], dma=("cast", i % 8))


class WB:
    def __init__(self, nc, bounds):
        self.bounds = bounds
        self.parts = [nc.dram_tensor("wallb%d" % i, [128, bounds[i + 1] - bounds[i]], BF16, kind="Internal").ap()
                      for i in range(len(bounds) - 1)]

    def __getitem__(self, key):
        rows, cols = key
        for i in range(len(self.parts)):
            if self.bounds[i] <= cols.start and cols.stop <= self.bounds[i + 1]:
                return self.parts[i][rows, cols.start - self.bounds[i]:cols.stop - self.bounds[i]]
        raise ValueError("weight slice straddles scratch tensors: %r" % (cols,))


def dram(nc, name, shape, dt, kind):
    return nc.dram_tensor(name, list(shape), dt, kind=kind).ap()


def gla_inproj_tile(P, C, G, wallb, offs, ti, gl, need_qr):
    t0 = ti * T
    NW = C.NW
    if need_qr:
        P.store(G.x1T[:, :, t0:t0 + T].rearrange("k p t -> p k t"), C.xs[:], "st_x1", ["xs"], [("x1T", ti)])
    oq = offs["gqkr"]
    ov = offs["gv"]
    js = list(range(32)) if need_qr else list(range(8, 16))

    def load_q(i):
        s = i % NW
        j = js[i]
        P.load(C.wg[s][:], wallb[:, oq + j * 2048:oq + (j + 1) * 2048].rearrange("p (k c) -> p k c", k=KC),
               ("wg", s), [], [("wg", s)])

    def load_v(hh):
        s = hh % 2
        P.load(C.wd[s][:, 0:32, :], wallb[:, ov + hh * 4096:ov + (hh + 1) * 4096].rearrange("p (k c) -> p k c", k=32),
               ("wd", s), [], [("wd", s)])

    for i in range(NW - 1):
        load_q(i)
    load_v(0)
    load_v(1)
    prenorm(P, C, gl * 96 + 2 * 16)
    qk = C.hT[:, 0:16, :]
    rs = C.hT[:, 16:32, :]
    last = [C.hT[:, 32 + 4 * i:36 + 4 * i, :].rearrange("p a t -> p (a t)").bitcast(F32) for i in range(2)]
    for i, j in enumerate(js):
        if i + NW - 1 < len(js):
            load_q(i + NW - 1)
        s = i % NW
        b = i % 2
        for k in range(KC):
            P.mm(C.psA[b][:], C.wg[s][:, k, :], C.xn[:, k, :], k == 0, k == KC - 1, [("wg", s)] + XN, [("psA", b)])
        if j < 8:
            P.act(qk[:, j, :], C.psA[b][:], AF.Copy, [("psA", b)], ["h0"], scale=1.0 / 16.0)
        elif j < 16:
            P.act(qk[:, j, :], C.psA[b][:], AF.Copy, [("psA", b)], ["h0"])
        else:
            P.act(rs[:, j - 16, :], C.psA[b][:], AF.Silu, [("psA", b)], ["h1"])
        if j == 15:
            if need_qr:
                P.store(G.qT[:, :, t0:t0 + T].rearrange("k p t -> p k t"), qk[:, 0:8, :], "st_q", ["h0"], [("qT", ti)])
            P.store(G.kT[:, :, t0:t0 + T].rearrange("k p t -> p k t"), qk[:, 8:16, :], "st_k", ["h0"], [("kT", ti)])
    if need_qr:
        P.store(G.rsT[:, :, t0:t0 + T].rearrange("k p t -> p k t"), rs, "st_r", ["h1"], [("rsT", ti)])
    vst = C.vst
    for hh in range(8):
        s = hh % 2
        wv = C.wd[s][:, 0:32, :].rearrange("p (k a) c -> p k (a c)", k=KC)
        for tc in range(4):
            b = (hh * 4 + tc) % 2
            for k in range(KC):
                P.mm(C.psB[b][:, 0:256], C.xn[:, k, tc * 128:(tc + 1) * 128], wv[:, k, :], k == 0, k == KC - 1,
                     [("wd", s)] + XN, [("psB", b)])
            P.copy(vst[:, tc, hh * 256:(hh + 1) * 256], C.psB[b][:, 0:256], [("psB", b)], ["U1"])
        if hh + 2 < 8:
            load_v(hh + 2)
    P.store(G.v[ti * 4:(ti + 1) * 4].rearrange("c p f -> p c f"), vst, "st_v", ["U1"], [("v", ti)])
    for dr in range(2):
        for k in range(KC):
            P.mm(C.psC[dr][0:16, :], C.wz[:, k, dr * 16:(dr + 1) * 16], C.xn[:, k, :], k == 0, k == KC - 1,
                 ["wz"] + XN, [("psC", dr)])
        P.copy(C.zaug[dr][0:16, :], C.psC[dr][0:16, :], [("psC", dr)], [("zaug", dr)])
    for dr in range(2):
        for tc in range(4):
            sl = (dr * 4 + tc) % 2
            for hf in range(2):
                b = hf
                P.mm(C.psA[b][:], C.zaug[dr][0:17, tc * 128:(tc + 1) * 128], C.wup[0:17, dr, hf * 512:(hf + 1) * 512],
                     True, True, [("zaug", dr), "wup"], [("psA", b)])
                P.act(C.tmpc[:, b, :], C.psA[b][:], AF.Exp, [("psA", b)], [("tmpc", b)], scale=-1.0)
                P.act(last[sl][:, hf * 512:(hf + 1) * 512], C.tmpc[:, b, :], AF.Ln, [("tmpc", b), "oneT"], [("h2", sl)],
                      bias=C.oneT[:], scale=1.0)
            P.store(G.la[dr, ti * 4 + tc], last[sl], ("st_la", sl), [("h2", sl)], [("la", dr, ti * 4 + tc)])


def alloc_gla_in(nc, st, C):
    C.wz = st.enter_context(nc.sbuf_tensor(_u("wz"), [128, KC, 32], BF16))
    C.zaug = [st.enter_context(nc.sbuf_tensor(_u("zaug%d" % i), [17, T], BF16)) for i in range(2)]
    C.wup = st.enter_context(nc.sbuf_tensor(_u("wup"), [17, 2, 1024], BF16))
    C.oneT = st.enter_context(nc.sbuf_tensor(_u("oneT"), [128, 1], F32))


def init_gla_in(P, C, wallb, offs):
    P.memset(C.oneT[:], 1.0, ["oneT"])
    for dr in range(2):
        P.memset(C.zaug[dr][:], 1.0, [("zaug", dr)])
    P.load(C.wup[:], wallb[0:17, offs["gwup"]:offs["gwup"] + 2048].rearrange("p (d f) -> p d f", d=2), "wupl", [], ["wup"])
    P.load(C.wz[:], wallb[:, offs["gz"]:offs["gz"] + 512].rearrange("p (k c) -> p k c", k=KC), "wz", [], ["wz"])


def alloc_scan(nc, st, S):
    S.Sf = st.enter_context(nc.sbuf_tensor(_u("Sf"), [128, 4, 2, 512], F32))
    S.Sb = st.enter_context(nc.sbuf_tensor(_u("Sb"), [128, 4, 2, 512], BF16))
    S.keep = st.enter_context(nc.sbuf_tensor(_u("keep"), [128, 8], F32))
    S.qc = [st.enter_context(nc.sbuf_tensor(_u("qc%d" % i), [128, 8, CH], BF16)) for i in range(2)]
    S.kc = [st.enter_context(nc.sbuf_tensor(_u("kc%d" % i), [128, 8, CH], BF16)) for i in range(2)]
    S.vc = [st.enter_context(nc.sbuf_tensor(_u("vc%d" % i), [128, 2048], BF16)) for i in range(2)]
    S.lac = [st.enter_context(nc.sbuf_tensor(_u("lac%d" % i), [128, 1024], F32)) for i in range(2)]
    S.ofc = [st.enter_context(nc.sbuf_tensor(_u("ofc%d" % i), [128, 16, CH], F32)) for i in range(2)]
    S.ost = [st.enter_context(nc.sbuf_tensor(_u("ost%d" % i), [128, 16, CH], F32)) for i in range(2)]
    S.E1 = st.enter_context(nc.sbuf_tensor(_u("E1"), [128, 8, CH], F32))
    S.E2 = st.enter_context(nc.sbuf_tensor(_u("E2"), [128, 8, CH], F32))
    S.qin = st.enter_context(nc.sbuf_tensor(_u("qin"), [128, 8, CH], BF16))
    S.kin = st.enter_context(nc.sbuf_tensor(_u("kin"), [128, 8, CH], BF16))
    S.kend = st.enter_context(nc.sbuf_tensor(_u("kend"), [128, 8, CH], BF16))
    S.ke = st.enter_context(nc.sbuf_tensor(_u("ke"), [128, 8, CH], BF16))
    S.sm = st.enter_context(nc.sbuf_tensor(_u("sm"), [128, 4, CH], BF16))
    S.dec = st.enter_context(nc.sbuf_tensor(_u("dec"), [128, 8], F32))
    S.tri = st.enter_context(nc.sbuf_tensor(_u("tri"), [128, 2, CH], F32))
    S.msk = st.enter_context(nc.sbuf_tensor(_u("msk"), [128, 2, 4, CH], F32))
    S.ident = st.enter_context(nc.sbuf_tensor(_u("ident"), [128, CH], BF16))
    S.identf = st.enter_context(nc.sbuf_tensor(_u("identf"), [128, CH], F32))
    S.psb = st.enter_context(nc.psum_tensor(_u("psb"), [128, 8, CH], F32))
    S.pssc = st.enter_context(nc.psum_tensor(_u("pssc"), [128, 4, CH], F32))
    S.pstr = st.enter_context(nc.psum_tensor(_u("pstr"), [128, 8, CH], BF16))
    S.pso = [st.enter_context(nc.psum_tensor(_u("pso%d" % i), [128, 4, CH], F32)) for i in range(2)]
    S.pssu = [st.enter_context(nc.psum_tensor(_u("pssu%d" % i), [128, 512], F32)) for i in range(2)]


def init_scan(P, S, consts):
    P.load(S.tri[:], consts["tri"], "c_tri", [], ["tri"])
    P.load(S.msk[:], consts["msk"], "c_msk", [], ["msk"])
    P.load(S.identf[:], consts["ident"], "c_id", [], ["identf"])
    P.load(S.keep[:], consts["keep"], "c_keep", [], ["keep"])
    P.copy(S.ident[:], S.identf[:], ["identf"], ["ident"])


def scan_dir(P, S, G, dr, seq, resets):
    lastcol = CH - 1 if dr == 0 else 0
    P.memset(S.Sf[:], 0.0, [("Sf", h) for h in range(4)])
    sb_valid = False
    n = len(seq)

    def loads(i):
        c, full = seq[i]
        sl = i % 2
        c0 = c * CH
        P.load(S.kc[sl][:], G.kT[:, :, c0:c0 + CH].rearrange("k p t -> p k t"), ("ld_k", sl), [("kT", c // 4)], [("kc", sl)])
        P.load(S.vc[sl][:], G.v[c], ("ld_v", sl), [("v", c // 4)], [("vc", sl)])
        P.load(S.lac[sl][:], G.la[dr, c], ("ld_la", sl), [("la", dr, c)], [("lac", sl)])
        if full:
            P.load(S.qc[sl][:], G.qT[:, :, c0:c0 + CH].rearrange("k p t -> p k t"), ("ld_q", sl), [("qT", c // 4)], [("qc", sl)])
            if dr == 1:
                P.load(S.ofc[sl][:], G.oT[:, :, c0:c0 + CH].rearrange("k p t -> p k t"), ("ld_of", sl), [("oT", c)], [("ofc", sl)])

    loads(0)
    for i in range(n):
        c, full = seq[i]
        sl = i % 2
        c0 = c * CH
        if i + 1 < n:
            loads(i + 1)
        if i in resets:
            col = resets[i]
            for h in range(4):
                P.ts(S.Sf[:, h], S.Sf[:, h], S.keep[:, col:col + 1], None, ALU.mult, None, [("Sf", h), "keep"], [("Sf", h)])
            sb_valid = False
        if full and not sb_valid:
            P.act(S.Sb[:], S.Sf[:], AF.Copy, [("Sf", h) for h in range(4)], [("Sb", h) for h in range(4)])
            sb_valid = True
        for fc in range(8):
            P.mm(S.psb[:, fc, :], S.lac[sl][:, fc * 128:(fc + 1) * 128], S.tri[:, dr, :], True, True,
                 [("lac", sl), "tri"], ["psb"])
        P.act(S.E2[:], S.psb[:], AF.Exp, ["psb"], ["E2"], scale=-1.0)
        P.act(S.dec[:], S.psb[:, :, lastcol], AF.Exp, ["psb"], ["dec"])
        if full:
            P.act(S.E1[:], S.psb[:], AF.Exp, ["psb"], ["E1"])
            P.tt(S.qin[:], S.qc[sl][:], S.E1[:], ALU.mult, [("qc", sl), "E1"], ["qin"])
        P.tt(S.kin[:], S.kc[sl][:], S.E2[:], ALU.mult, [("kc", sl), "E2"], ["kin"])
        for fc in range(8):
            P.ts(S.kend[:, fc, :], S.kin[:, fc, :], S.dec[:, fc:fc + 1], None, ALU.mult, None, ["kin", "dec"], ["kend"])
        for fc in range(8):
            P.tr(S.pstr[:, fc, :], S.kend[:, fc, :], S.ident[:], ["kend", "ident"], ["pstr"])
        P.copy(S.ke[:], S.pstr[:], ["pstr"], ["ke"])
        if full:
            for h in range(4):
                for dc in range(2):
                    f = 2 * h + dc
                    P.mm(S.pssc[:, h, :], S.kin[:, f, :], S.qin[:, f, :], dc == 0, dc == 1, ["kin", "qin"], ["pssc"])
            P.tt(S.sm[:], S.pssc[:], S.msk[:, dr], ALU.mult, ["pssc", "msk"], ["sm"])
        for h in range(4):
            if full:
                ob = h % 2
                for j in range(4):
                    e0 = h * 512 + j * 128
                    P.mm(S.pso[ob][:, j, :], S.vc[sl][:, e0:e0 + 128], S.sm[:, h, :], True, False,
                         [("vc", sl), "sm"], [("pso", ob)])
                    for dc in range(2):
                        P.mm(S.pso[ob][:, j, :], S.Sb[:, h, dc, j * 128:(j + 1) * 128], S.qin[:, 2 * h + dc, :], False, dc == 1,
                             [("Sb", h), "qin"], [("pso", ob)])
                if dr == 0:
                    P.act(S.ost[sl][:, 4 * h:4 * h + 4, :], S.pso[ob][:], AF.Copy, [("pso", ob)], [("ost", sl)])
                else:
                    P.tt(S.ost[sl][:, 4 * h:4 * h + 4, :], S.pso[ob][:], S.ofc[sl][:, 4 * h:4 * h + 4, :], ALU.add,
                         [("pso", ob), ("ofc", sl)], [("ost", sl)])
            for dc in range(2):
                P.mm(S.pssu[dc][:], S.ke[:, 2 * h + dc, :], S.vc[sl][:, h * 512:(h + 1) * 512], True, True,
                     ["ke", ("vc", sl)], [("pssu", dc)])
            for dc in range(2):
                P.stt(S.Sf[:, h, dc, :], S.Sf[:, h, dc, :], S.dec[:, 2 * h + dc:2 * h + dc + 1], S.pssu[dc][:],
                      ALU.mult, ALU.add, [("Sf", h), "dec", ("pssu", dc)], [("Sf", h)])
            if full:
                P.act(S.Sb[:, h], S.Sf[:, h], AF.Copy, [("Sf", h)], [("Sb", h)])
        if not full:
            sb_valid = False
        if full:
            P.store(G.oT[:, :, c0:c0 + CH].rearrange("k p t -> p k t"), S.ost[sl][:], ("st_o", sl), [("ost", sl)], [("oT", c)])


def gla_out_tile(P, C, G, wallb, offs, ti):
    t0 = ti * T
    NW = C.NW
    oo = offs["go"]

    def load_o(m):
        s = m % NW
        P.load(C.wg[s][:], wallb[:, oo + m * 2048:oo + (m + 1) * 2048].rearrange("p (k c) -> p k c", k=KC),
               ("wg", s), [], [("wg", s)])

    for m in range(NW - 1):
        load_o(m)
    P.load(C.xs[:], G.x1T[:, :, t0:t0 + T].rearrange("k p t -> p k t"), "ld_xs", [("x1T", ti)], ["xs"])
    P.load(C.y, G.oT[:, :, t0:t0 + T].rearrange("k p t -> p k t"), "ld_o", [("oT", c) for c in range(ti * 4, ti * 4 + 4)], YY)
    rs = C.hT[:, 16:32, :]
    P.load(rs, G.rsT[:, :, t0:t0 + T].rearrange("k p t -> p k t"), "ld_rs", [("rsT", ti)], ["h1"])
    P.act(C.sq[:, 0:KC, :], C.y, AF.Square, YY, ["h0"])
    for h in range(4):
        for j in range(4):
            P.mm(C.pss[:], C.ones[:], C.sq[:, 4 * h + j, :], j == 0, j == 3, ["h0", "ones"], ["pss"])
        rstd_from_pss(P, C, 512)
        for j in range(4):
            c = 4 * h + j
            b = c % 2
            P.stt(C.tmpc[:, b, :], C.y[:, c, :], C.hg[:, j:j + 1], C.rstd[:], ALU.mult, ALU.mult,
                  YY + ["hg", "rstd"], [("tmpc", b)])
            P.tt(rs[:, c, :], C.tmpc[:, b, :], rs[:, c, :], ALU.mult, [("tmpc", b), "h1"], ["h1"])
    pending = None
    for m in range(KC):
        if m + NW - 1 < KC:
            load_o(m + NW - 1)
        s = m % NW
        b = m % 2
        for k in range(KC):
            P.mm(C.psC[b][:], C.wg[s][:, k, :], rs[:, k, :], k == 0, k == KC - 1, [("wg", s), "h1"], [("psC", b)])
        pending = evac_y_chunk(P, C, C.psC[b][:], ("psC", b), m, 0 * 96 + 3 * 16, pending)
    finish_ssq(P, C, pending)
    postnorm_residual(P, C, False)


def pool_tile(P, C, G, wallb, offs, ti):
    t0 = (ti + 1) * T
    P.load(C.xs[:], G.x4e[:, :, t0:t0 + T].rearrange("k p t -> p k t"), "ld_xs", [("x4e", ti + 1)], ["xs"])
    P.load(C.hp, G.h1e[:, :, t0 - 8:t0 + T + 8].rearrange("k p t -> p k t"), "ld_hp",
           [("h1e", ti), ("h1e", ti + 1), ("h1e", ti + 2)], YY)
    icnt = C.wd[1][:, 0:32, :].rearrange("p a c -> p (a c)").bitcast(F32).rearrange("p (g t) -> p g t", g=4)
    P.load(icnt.rearrange("p (o g) t -> p o (g t)", o=1), G.icnt[ti].partition_broadcast(128), ("wd", 1), [], [("wd", 1)])
    wslots = [C.wg[0], C.wg[1], C.wu[0], C.wu[1]]
    wnames = [("wg", 0), ("wg", 1), ("wu", 0), ("wu", 1)]
    for q in range(4):
        P.load(wslots[q][:], wallb[:, offs["pw"] + q * 2048:offs["pw"] + (q + 1) * 2048].rearrange("p (k c) -> p k c", k=KC),
               wnames[q], [], [wnames[q]])
    hp = C.hp
    pflat = C.hT[:, 16:44, :].rearrange("p a t -> p (a t)").bitcast(F32)
    PW = T + 16
    pa = pflat[:, 0:4 * PW].rearrange("p (c t) -> p c t", c=4)
    pb = pflat[:, 4 * PW:8 * PW].rearrange("p (c t) -> p c t", c=4)
    for g in range(4):
        cs = slice(4 * g, 4 * g + 4)
        if g == 0:
            P.tt(pa[:, :, 0:T], hp[:, cs, 7:7 + T], hp[:, cs, 8:8 + T], ALU.add, YY, HR)
            ssum = pa
        else:
            P.tt(pa[:, :, 0:T + 15], hp[:, cs, 0:T + 15], hp[:, cs, 1:T + 16], ALU.add, YY, HR)
            if g == 1:
                P.tt(pb[:, :, 0:T], pa[:, :, 6:6 + T], pa[:, :, 8:8 + T], ALU.add, HR, HR)
                ssum = pb
            else:
                P.tt(pb[:, :, 0:T + 13], pa[:, :, 0:T + 13], pa[:, :, 2:T + 15], ALU.add, HR, HR)
                if g == 2:
                    P.tt(pa[:, :, 0:T], pb[:, :, 4:4 + T], pb[:, :, 8:8 + T], ALU.add, HR, HR)
                    ssum = pa
                else:
                    P.tt(pa[:, :, 0:T + 9], pb[:, :, 0:T + 9], pb[:, :, 4:T + 13], ALU.add, HR, HR)
                    P.tt(pb[:, :, 0:T], pa[:, :, 0:T], pa[:, :, 8:8 + T], ALU.add, HR, HR)
                    ssum = pb
        for cc in range(4):
            c = 4 * g + cc
            b = c % 2
            P.tt(C.tmpc[:, b, :], ssum[:, cc, 0:T], icnt[:, g, :], ALU.mult, HR + [("wd", 1)], [("tmpc", b)])
            P.tt(C.hT[:, c, :], C.tmpc[:, b, :], hp[:, c, 8:8 + T], ALU.subtract, [("tmpc", b)] + YY, ["h0"])
    pending = None
    g3 = 1 * 96 + 3 * 16
    for oc in range(KC):
        g = oc // 4
        b = oc % 2
        q = oc // 4
        for cc in range(4):
            P.mm(C.psC[b][:], wslots[q][:, (oc % 4) * 4 + cc, :], C.hT[:, 4 * g + cc, :], cc == 0, cc == 3,
                 [wnames[q], "h0"], [("psC", b)])
        P.ts(C.sg[:, b, :], C.psC[b][:], C.pbc[:, oc:oc + 1], C.psc[:, oc:oc + 1], ALU.add, ALU.mult,
             [("psC", b), "pbc", "psc"], [("tmpc", b)])
        P.act(C.y[:, oc, :], C.sg[:, b, :], AF.Copy, [("tmpc", b), "gc"], YY, scale=C.gc[:, g3 + oc:g3 + oc + 1])
        P.act(C.sqc[:, b, :], C.sg[:, b, :], AF.Square, [("tmpc", b)], [("sqc", b)])
        if pending is not None:
            pm = pending
            P.mm(C.pss[:], C.ones[:], C.sqc[:, pm % 2, :], pm == 0, False, [("sqc", pm % 2), "ones"], ["pss"])
        pending = oc
    finish_ssq(P, C, pending)
    postnorm_residual(P, C, False)


def build_fused(offs, ncols, first_cols):
    nc = bass.Bass("TRN2", target_bir_lowering=False)
    EI, EO, IN = "ExternalInput", "ExternalOutput", "Internal"
    wall = dram(nc, "wall", [128, ncols], F32, EI)
    wallb = WB(nc, [0, first_cols, offs["pw"], ncols])
    gcols = dram(nc, "gcols", [128, 192], F32, EI)
    xT = dram(nc, "xT", [KC, 128, LS], F32, EI)
    consts = {"tri": dram(nc, "tri", [128, 2, CH], F32, EI), "msk": dram(nc, "msk", [128, 2, 4, CH], F32, EI),
              "ident": dram(nc, "ident", [128, CH], F32, EI), "keep": dram(nc, "keep", [128, 8], F32, EI)}
    hgain = dram(nc, "hgain", [128, 4], F32, EI)
    hmask = dram(nc, "hmask", [128, 2], F32, EI)
    pbc = dram(nc, "pbc", [128, 16], F32, EI)
    psc = dram(nc, "psc", [128, 16], F32, EI)
    outT = dram(nc, "outT", [KC, 128, NT], F32, EO)
    G = Ctx()
    G.icnt = dram(nc, "icnt", [NTILE, 1, 4 * T], F32, EI)
    G.x1T = dram(nc, "x1T", [KC, 128, LS], F32, IN)
    G.qT = dram(nc, "qT", [8, 128, LS], BF16, IN)
    G.kT = dram(nc, "kT", [8, 128, LS], BF16, IN)
    G.rsT = dram(nc, "rsT", [KC, 128, LS], BF16, IN)
    G.v = dram(nc, "v", [NCS, 128, 2048], BF16, IN)
    G.la = dram(nc, "la", [2, NCS, 128, 1024], F32, IN)
    G.oT = dram(nc, "oT", [KC, 128, LS], F32, IN)
    G.x4e = dram(nc, "x4e", [KC, 128, NEXT * T], F32, IN)
    G.h1e = dram(nc, "h1e", [KC, 128, NEXT * T], F32, IN)

    ext_tiles = [NTS - NTILE - 1] + list(range(NTS - NTILE, NTS)) + [0]
    qr_tiles = set(ext_tiles)
    full_chunks = set()
    for t in ext_tiles:
        full_chunks.update(range(4 * t, 4 * t + 4))

    with ExitStack() as gstack:
        P = Prog(nc, gstack)
        cast_cols(P, wall, wallb, 0, first_cols)
        P.emit()

        with ExitStack() as st:
            C = Ctx()
            alloc_common(nc, st, C)
            alloc_ffn_bufs(nc, st, C)
            alloc_gla_in(nc, st, C)
            cast_cols(P, wall, wallb, first_cols, ncols)
            init_common(P, C, gcols)
            init_gla_in(P, C, wallb, offs)
            for ti in range(NTS):
                t0 = ti * T
                P.load(C.xs[:], xT[:, :, t0:t0 + T].rearrange("k p t -> p k t"), "ld_xs", [], ["xs"])
                ffn(P, C, wallb, offs["f00g"], offs["f00u"], offs["f00d"], 0 * 96 + 0 * 16, 0 * 96 + 1 * 16)
                gla_inproj_tile(P, C, G, wallb, offs, ti, 0, ti in qr_tiles)
            P.emit()

        with ExitStack() as st:
            S = Ctx()
            alloc_scan(nc, st, S)
            init_scan(P, S, consts)
            SEGC = NCS // 4
            seq = [(c, c in full_chunks and c >= SEGC) for c in range(NCS)] + [(c, True) for c in range(4)]
            resets = {SEGC: 0, 2 * SEGC: 1, 3 * SEGC: 2, NCS: 3}
            scan_dir(P, S, G, 0, seq, resets)
            seq = []
            for sg in (2, 1, 0):
                seq += [(c, sg == 0 and c in full_chunks) for c in range((sg + 1) * SEGC - 1, sg * SEGC - 1, -1)]
            seq += [(c, True) for c in range(NCS - 1, 3 * SEGC - 1, -1)]
            seq += [(c, True) for c in range(3 * SEGC - 1, 3 * SEGC - 5, -1)]
            resets = {SEGC: 4, 2 * SEGC: 5, 3 * SEGC: 6, NCS: 7}
            scan_dir(P, S, G, 1, seq, resets)
            P.emit()

        with ExitStack() as st:
            C = Ctx()
            alloc_common(nc, st, C)
            alloc_ffn_bufs(nc, st, C)
            C.hg = st.enter_context(nc.sbuf_tensor(_u("hg"), [128, 4], F32))
            C.hm = st.enter_context(nc.sbuf_tensor(_u("hm"), [128, 2], F32))
            init_common(P, C, gcols)
            P.load(C.hg[:], hgain, "ld_hg", [], ["hg"])
            P.load(C.hm[:], hmask, "ld_hm", [], ["hm"])
            for e, ti in enumerate(ext_tiles):
                e0 = e * T
                gla_out_tile(P, C, G, wallb, offs, ti)
                ffn(P, C, wallb, offs["f01g"], offs["f01u"], offs["f01d"], 0 * 96 + 4 * 16, 0 * 96 + 5 * 16)
                ffn(P, C, wallb, offs["f10g"], offs["f10u"], offs["f10d"], 1 * 96 + 0 * 16, 1 * 96 + 1 * 16)
                if 1 <= e <= NTILE:
                    P.store(G.x4e[:, :, e0:e0 + T].rearrange("k p t -> p k t"), C.xs[:], "st_x4", ["xs"], [("x4e", e)])
                ssq_xs(P, C)
                rstd_from_pss(P, C, D)
                gp = 1 * 96 + 2 * 16
                for k in range(KC):
                    P.stt(C.y[:, k, :], C.xs[:, k, :], C.gc[:, gp + k:gp + k + 1], C.rstd[:], ALU.mult, ALU.mult,
                          ["xs", "gc", "rstd"], YY)
                if e == 0 or e == NEXT - 1:
                    col = 0 if e == 0 else 1
                    P.ts(C.y, C.y, C.hm[:, col:col + 1], None, ALU.mult, None, YY + ["hm"], YY)
                P.store(G.h1e[:, :, e0:e0 + T].rearrange("k p t -> p k t"), C.y, "st_h1", YY, [("h1e", e)])
            P.emit()

        with ExitStack() as st:
            C = Ctx()
            alloc_common(nc, st, C)
            alloc_ffn_bufs(nc, st, C)
            C.pbc = st.enter_context(nc.sbuf_tensor(_u("pbc"), [128, 16], F32))
            C.psc = st.enter_context(nc.sbuf_tensor(_u("psc"), [128, 16], F32))
            init_common(P, C, gcols)
            P.load(C.pbc[:], pbc, "ld_pb", [], ["pbc"])
            P.load(C.psc[:], psc, "ld_ps", [], ["psc"])
            for ti in range(NTILE):
                t0 = ti * T
                pool_tile(P, C, G, wallb, offs, ti)
                ffn(P, C, wallb, offs["f11g"], offs["f11u"], offs["f11d"], 1 * 96 + 4 * 16, 1 * 96 + 5 * 16)
                P.store(outT[:, :, t0:t0 + T].rearrange("k p t -> p k t"), C.xs[:], "st_out", ["xs"], [("outT", ti)])
            P.emit()
        nsem = P.nsem
    return nc, nsem


def _cols(vec, n=16):
    return np.ascontiguousarray(np.asarray(vec, np.float32).reshape(n, 128).T)


def _consts():
    s = np.arange(CH)[:, None]
    c = np.arange(CH)[None, :]
    tri = np.zeros((128, 2, CH), np.float32)
    tri[:, 0, :] = np.where(s <= c, -1.0 / 16.0, 0.0)
    tri[:, 1, :] = np.where(s >= c, -1.0 / 16.0, 0.0)
    msk = np.zeros((128, 2, 4, CH), np.float32)
    msk[:, 0] = np.where(c >= s, 1.0, 0.0)[:, None, :]
    msk[:, 1] = np.where(c <= s, 1.0, 0.0)[:, None, :]
    ident = np.eye(128, dtype=np.float32)
    return tri, msk, ident


def kernel(x, ffn_w_gate, ffn_w_up, ffn_w_down, norm_gain, gla_w_in, gla_w_gate_up, gla_b_gate,
           gla_head_gain, gla_w_out, pool_w, pool_b, pool_scale):
    x = np.asarray(x, np.float32)
    B, L, _ = x.shape
    cores = list(range(NCORES))
    gcols = np.concatenate([_cols(norm_gain[li, n]) for li in range(2) for n in range(6)], axis=1)
    tri, msk, ident = _consts()

    pk = Pack()
    pack_ffn(pk, "f00", ffn_w_gate[0, 0], ffn_w_up[0, 0], ffn_w_down[0, 0])
    pack_gla_in(pk, gla_w_in[0], gla_w_gate_up[0], gla_b_gate[0])
    first_cols = pk.n + ((-pk.n) % 4096)
    if first_cols != pk.n:
        pk.add("pad0", np.zeros((128, first_cols - pk.n), np.float32))
    pk.add("go", _lhs_layout(gla_w_out[0], KC))
    pack_ffn(pk, "f01", ffn_w_gate[0, 1], ffn_w_up[0, 1], ffn_w_down[0, 1])
    pack_ffn(pk, "f10", ffn_w_gate[1, 0], ffn_w_up[1, 0], ffn_w_down[1, 0])
    pw = pool_w[0]
    wp = pw.reshape(4, 4, 128, 4, 128)
    wp = np.ascontiguousarray(wp.transpose(2, 0, 3, 1, 4)).reshape(128, -1)
    pk.add("pw", wp)
    pack_ffn(pk, "f11", ffn_w_gate[1, 1], ffn_w_up[1, 1], ffn_w_down[1, 1])
    wall = pk.build()
    nc, _ = build_fused(pk.offs, pk.n, first_cols)

    hgain = _cols(gla_head_gain[0], 4)
    pbc = _cols(pool_b[0])
    pscl = _cols(pool_scale[0])
    tpos = np.arange(NT)
    xTb = [np.ascontiguousarray(x[b].T) for b in range(B)]
    in_maps = []
    for c in cores:
        b, s = divmod(c, 4)
        order = [(s + 1 + i) % 4 for i in range(4)]
        xT = np.concatenate([xTb[b][:, g * NT:(g + 1) * NT] for g in order], axis=1).reshape(KC, 128, LS)
        keep = np.ones((128, 8), np.float32)
        for i in (1, 2, 3):
            keep[:, i - 1] = 0.0 if order[i] == 0 else 1.0
        keep[:, 3] = 0.0 if order[0] == 0 else 1.0
        prev = [order[2], order[1], order[0], order[3]]
        for i in range(4):
            keep[:, 4 + i] = 0.0 if prev[i] == 0 else 1.0
        hmask = np.ones((128, 2), np.float32)
        hmask[:, 0] = 0.0 if s == 0 else 1.0
        hmask[:, 1] = 0.0 if s == 3 else 1.0
        icnt = np.zeros((4, NT), np.float32)
        tg = tpos + s * NT
        for g, w in enumerate((2, 4, 8, 16)):
            lo = np.clip(tg - w // 2, 0, L)
            hi = np.clip(tg + w // 2, 0, L)
            icnt[g] = 1.0 / (hi - lo).astype(np.float32)
        icnt = np.ascontiguousarray(icnt.reshape(4, NTILE, T).transpose(1, 0, 2)).reshape(NTILE, 1, 4 * T)
        in_maps.append({"wall": wall, "gcols": gcols, "xT": xT, "tri": tri, "msk": msk, "ident": ident, "keep": keep,
                        "hgain": hgain, "hmask": hmask, "pbc": pbc, "psc": pscl, "icnt": icnt})
    res = run_bass_kernel_spmd(nc, in_maps, core_ids=cores).results
    out = np.empty((B, L, D), np.float32)
    for c in cores:
        b, s = divmod(c, 4)
        out[b, s * NT:(s + 1) * NT, :] = res[c]["outT"].reshape(D, NT).T
    return out
```

```python
import os
from contextlib import ExitStack

import numpy as np
import concourse.bass as bass
import concourse.mybir as mybir
from concourse.bass_utils import run_bass_kernel_spmd

F32 = mybir.dt.float32
BF16 = mybir.dt.bfloat16
AF = mybir.ActivationFunctionType
ALU = mybir.AluOpType

NCORES = 8
D = 2048
KC = 16
FF = 5632
FC = 44
NT = 4096
T = 512
NTILE = NT // T
CH = 128
NCH = NT // CH
LS = 16384
NTS = LS // T
NCS = LS // CH
NEXT = NTILE + 2
EPS = 1e-6
SEM_MAX = 30000

ENGS = ["sync", "act", "dve", "pe", "pool"]
BLK = {"sync": "sync", "act": "scalar", "dve": "vector", "pe": "tensor", "pool": "gpsimd"}


class _Op:
    __slots__ = ("eng", "fn", "dma", "deps", "flag", "sig")


class _Chan:
    def __init__(self, prog, name):
        self.prog = prog
        self.name = name
        self.sem = None
        self.val = 0
        self.gen = 0

    def signal(self, inc):
        if self.sem is None or self.val + inc > SEM_MAX:
            self.sem = self.prog.new_sem("%s_%d" % (self.name, self.gen))
            self.gen += 1
            self.val = 0
        self.val += inc
        return (self.sem, self.val, inc)


class Prog:
    def __init__(self, nc, stack):
        self.nc = nc
        self.stack = stack
        self.nsem = 0
        self.chan = {}
        self.waited = {e: {} for e in ENGS}
        self.reset()

    def new_sem(self, name):
        self.nsem += 1
        return self.stack.enter_context(self.nc.semaphore("s%d_%s" % (self.nsem, name)))

    def reset(self):
        self.ops = {e: [] for e in ENGS}
        self.lastw = {}
        self.readers = {}
        self.dma_last = {}

    def _chan(self, key):
        c = self.chan.get(key)
        if c is None:
            c = self.chan[key] = _Chan(self, str(key).replace(" ", "").replace("'", "").replace("(", "").replace(")", "").replace(",", "_"))
        return c

    def add(self, eng, fn, reads=(), writes=(), dma=None):
        op = _Op()
        op.eng = eng
        op.fn = fn
        op.dma = dma
        op.flag = False
        op.sig = None
        deps = []
        for r in reads:
            w = self.lastw.get(r)
            if w is not None:
                deps.append(w)
        for r in writes:
            w = self.lastw.get(r)
            if w is not None:
                deps.append(w)
            deps.extend(self.readers.get(r, ()))
        dd = []
        seen = set()
        for d in deps:
            if id(d) in seen or d is op:
                continue
            seen.add(id(d))
            if d.dma is None and d.eng == eng and dma is None:
                continue
            d.flag = True
            dd.append(d)
        op.deps = dd
        if dma is not None:
            op.sig = self._chan(("dma", dma)).signal(16)
            self.dma_last[dma] = op
        for r in writes:
            self.lastw[r] = op
            self.readers[r] = []
        for r in reads:
            lst = self.readers.setdefault(r, [])
            if op.dma is None:
                lst[:] = [x for x in lst if not (x.dma is None and x.eng == eng)]
            lst.append(op)
        self.ops[eng].append(op)
        return op

    def emit(self, name=None):
        nc = self.nc
        finals = []
        for e in ENGS:
            lst = [o for o in self.ops[e] if o.dma is None]
            if lst:
                lst[-1].flag = True
            for o in self.ops[e]:
                if o.dma is None and o.flag:
                    o.sig = self._chan(("eng", e)).signal(1)
            if lst:
                finals.append(lst[-1].sig)
        for k, o in self.dma_last.items():
            finals.append(o.sig)
        prog = self

        def body_for(e):
            def body(eng):
                wd = prog.waited[e]

                def wait(sig):
                    sem, val, _ = sig
                    k = id(sem)
                    if wd.get(k, 0) >= val:
                        return
                    wd[k] = val
                    eng.wait_ge(sem, val)

                for op in prog.ops[e]:
                    for d in op.deps:
                        wait(d.sig)
                    ins = op.fn(eng)
                    if op.sig is not None:
                        ins.then_inc(op.sig[0], op.sig[2])
                for f in finals:
                    wait(f)
            return body

        with nc.Block() as block:
            for e in ENGS:
                getattr(block, BLK[e])(body_for(e))
        self.reset()

    def mm(self, out, lhsT, rhs, start, stop, reads, writes):
        return self.add("pe", lambda e: e.matmul(out, lhsT=lhsT, rhs=rhs, start=start, stop=stop), reads, writes)

    def tr(self, out, in_, ident, reads, writes):
        return self.add("pe", lambda e: e.transpose(out=out, in_=in_, identity=ident), reads, writes)

    def act(self, out, in_, func, reads, writes, **kw):
        return self.add("act", lambda e: e.activation(out=out, in_=in_, func=func, **kw), reads, writes)

    def stt(self, out, in0, scalar, in1, op0, op1, reads, writes):
        return self.add("dve", lambda e: e.scalar_tensor_tensor(out=out, in0=in0, scalar=scalar, in1=in1, op0=op0, op1=op1), reads, writes)

    def tt(self, out, in0, in1, op, reads, writes, eng="dve"):
        return self.add(eng, lambda e: e.tensor_tensor(out=out, in0=in0, in1=in1, op=op), reads, writes)

    def ts(self, out, in0, s1, s2, op0, op1, reads, writes):
        if s2 is None or op1 is None:
            return self.add("dve", lambda e: e.tensor_scalar(out=out, in0=in0, scalar1=s1, scalar2=None, op0=op0), reads, writes)
        return self.add("dve", lambda e: e.tensor_scalar(out=out, in0=in0, scalar1=s1, scalar2=s2, op0=op0, op1=op1), reads, writes)

    def recip(self, out, in_, reads, writes):
        return self.add("dve", lambda e: e.reciprocal(out=out, in_=in_), reads, writes)

    def copy(self, out, in_, reads, writes, eng="dve"):
        return self.add(eng, lambda e: e.tensor_copy(out=out, in_=in_), reads, writes)

    def memset(self, out, val, writes, eng="dve"):
        return self.add(eng, lambda e: e.memset(out, val), (), writes)

    def load(self, out, in_, key, reads, writes, **kw):
        return self.add("sync", lambda e: e.dma_start(out=out, in_=in_, **kw), reads, writes, dma=key)

    def store(self, out, in_, key, reads, writes, **kw):
        return self.add("pool", lambda e: e.dma_start(out=out, in_=in_, **kw), reads, writes, dma=key)


class Pack:
    def __init__(self):
        self.parts = []
        self.offs = {}
        self.n = 0

    def add(self, name, arr):
        arr = np.ascontiguousarray(arr, dtype=np.float32).reshape(128, -1)
        self.offs[name] = self.n
        self.parts.append(arr)
        self.n += arr.shape[1]

    def build(self):
        pad = (-self.n) % 4096
        if pad:
            self.parts.append(np.zeros((128, pad), np.float32))
            self.n += pad
        return np.ascontiguousarray(np.concatenate(self.parts, axis=1))


def _lhs_layout(w, nj):
    K, N = w.shape
    kk = K // 128
    a = w.reshape(kk, 128, nj, N // nj)
    return np.ascontiguousarray(a.transpose(1, 2, 0, 3)).reshape(128, -1)


def pack_ffn(pk, tag, wg, wu, wd):
    pk.add(tag + "g", _lhs_layout(wg, FC))
    pk.add(tag + "u", _lhs_layout(wu, FC))
    pk.add(tag + "d", _lhs_layout(wd, KC))


def pack_gla_in(pk, w_in, w_gate_up, b_gate):
    qkr = np.concatenate([w_in[:, 0:2048], w_in[:, 4096:6144]], axis=1)
    pk.add("gqkr", _lhs_layout(qkr, 32))
    wv = w_in[:, 2048:4096]
    pk.add("gv", _lhs_layout(wv, 8))
    pk.add("gz", _lhs_layout(w_in[:, 6144:6176], 1))
    wup = np.zeros((128, 2, 1024), np.float32)
    wup[0:16] = np.transpose(w_gate_up, (1, 0, 2))
    wup[16] = b_gate
    pk.add("gwup", wup)


class Ctx:
    pass


_UNIQ = [0]


def _u(name):
    _UNIQ[0] += 1
    return "%s_%d" % (name, _UNIQ[0])


def alloc_common(nc, st, C):
    C.ones = st.enter_context(nc.sbuf_tensor(_u("ones"), [128, 128], BF16))
    C.epsT = st.enter_context(nc.sbuf_tensor(_u("epsT"), [128, 1], F32))
    C.eps4T = st.enter_context(nc.sbuf_tensor(_u("eps4T"), [128, 1], F32))
    C.gc = st.enter_context(nc.sbuf_tensor(_u("gc"), [128, 192], F32))


def init_common(P, C, gcols):
    P.memset(C.ones[:], 1.0, ["ones"])
    P.memset(C.epsT[:], EPS, ["epsT"])
    P.memset(C.eps4T[:], 4.0 * EPS, ["epsT"])
    P.load(C.gc[:], gcols, "gc", [], ["gc"])


def alloc_ffn_bufs(nc, st, C):
    C.xs = st.enter_context(nc.sbuf_tensor(_u("xs"), [128, KC, T], F32))
    C.U = st.enter_context(nc.sbuf_tensor(_u("U"), [128, KC * (T + 16)], F32))
    C.y = C.U[:, 0:KC * T].rearrange("p (k t) -> p k t", k=KC)
    C.xn = C.U[:, 0:KC * T // 2].bitcast(BF16).rearrange("p (k t) -> p k t", k=KC)
    C.vst = C.U[:, KC * T // 2:KC * T].bitcast(BF16).rearrange("p (c f) -> p c f", c=4)
    C.hp = C.U[:, :].rearrange("p (k t) -> p k t", k=KC)
    C.hT = st.enter_context(nc.sbuf_tensor(_u("hT"), [128, FC, T], BF16))
    C.sq = C.hT
    C.rstd = st.enter_context(nc.sbuf_tensor(_u("rstd"), [128, T], F32))
    C.tmp = st.enter_context(nc.sbuf_tensor(_u("tmp"), [128, T], F32))
    C.tmpc = st.enter_context(nc.sbuf_tensor(_u("tmpc"), [128, 2, T], F32))
    C.sg = C.tmpc
    C.sqc = st.enter_context(nc.sbuf_tensor(_u("sqc"), [128, 2, T], BF16))
    C.NW = 2
    C.wg = [st.enter_context(nc.sbuf_tensor(_u("wg%d" % i), [128, KC, 128], BF16)) for i in range(C.NW)]
    C.wu = [st.enter_context(nc.sbuf_tensor(_u("wu%d" % i), [128, KC, 128], BF16)) for i in range(C.NW)]
    C.wd = [st.enter_context(nc.sbuf_tensor(_u("wd%d" % i), [128, FC, 128], BF16)) for i in range(2)]
    C.psA = [st.enter_context(nc.psum_tensor(_u("psA%d" % i), [128, T], F32)) for i in range(2)]
    C.psB = [st.enter_context(nc.psum_tensor(_u("psB%d" % i), [128, T], F32)) for i in range(2)]
    C.psC = [st.enter_context(nc.psum_tensor(_u("psC%d" % i), [128, T], F32)) for i in range(2)]
    C.pss = st.enter_context(nc.psum_tensor(_u("pss"), [128, T], F32))


XN = ["U0"]
YY = ["U0", "U1"]


def hreg(j):
    if j < 16:
        return "h0"
    if j < 32:
        return "h1"
    return ("h2", (j - 32) // 4)


HR = ["h1", ("h2", 0), ("h2", 1), ("h2", 2)]


XS = [("xs", k) for k in range(KC)]


def rstd_from_pss(P, C, dn, half=False, W=T):
    f2 = 4.0 if half else 1.0
    P.act(C.tmp[:, 0:W], C.pss[:, 0:W], AF.Sqrt, ["pss", "epsT"], ["tmp"], scale=f2 / dn,
          bias=(C.eps4T[:] if half else C.epsT[:]))
    P.recip(C.rstd[:, 0:W], C.tmp[:, 0:W], ["tmp"], ["rstd"])


def ssq_xs(P, C, W=T):
    for k in range(KC):
        P.act(C.sq[:, k, 0:W], C.xs[:, k, 0:W], AF.Square, [("xs", k)], [("sq", k)])
        P.mm(C.pss[:, 0:W], C.ones[:], C.sq[:, k, 0:W], k == 0, k == KC - 1, [("sq", k), "ones"], ["pss"])


def prenorm(P, C, gcol0, W=T):
    ssq_xs(P, C, W)
    rstd_from_pss(P, C, D, False, W)
    for k in range(KC):
        P.stt(C.xn[:, k, 0:W], C.xs[:, k, 0:W], C.gc[:, gcol0 + k:gcol0 + k + 1], C.rstd[:, 0:W], ALU.mult, ALU.mult,
              [("xs", k), "gc", "rstd"], XN)


def postnorm_residual(P, C, half, W=T):
    rstd_from_pss(P, C, D, half, W)
    for m in range(KC):
        b = m % 2
        P.tt(C.tmpc[:, b, 0:W], C.y[:, m, 0:W], C.rstd[:, 0:W], ALU.mult, YY + ["rstd"], [("tmpc", b)])
        P.tt(C.xs[:, m, 0:W], C.xs[:, m, 0:W], C.tmpc[:, b, 0:W], ALU.add, [("xs", m), ("tmpc", b)], [("xs", m)])


def evac_y_chunk(P, C, ps, psname, m, gcol0, pending, W=T):
    b = m % 2
    P.act(C.y[:, m, 0:W], ps, AF.Copy, [psname, "gc"], YY, scale=C.gc[:, gcol0 + m:gcol0 + m + 1])
    P.act(C.sqc[:, b, 0:W], ps, AF.Square, [psname], [("sqc", b)])
    if pending is not None:
        pm = pending
        P.mm(C.pss[:, 0:W], C.ones[:], C.sqc[:, pm % 2, 0:W], pm == 0, False, [("sqc", pm % 2), "ones"], ["pss"])
    return m


def finish_ssq(P, C, pending, W=T):
    pm = pending
    P.mm(C.pss[:, 0:W], C.ones[:], C.sqc[:, pm % 2, 0:W], pm == 0, True, [("sqc", pm % 2), "ones"], ["pss"])


def ffn(P, C, wallb, og, ou, od, gpre, gpost, W=T):
    NW = C.NW

    def load_gu(j):
        s = j % NW
        P.load(C.wg[s][:], wallb[:, og + j * 2048:og + (j + 1) * 2048].rearrange("p (k c) -> p k c", k=KC),
               ("wg", s), [], [("wg", s)])
        P.load(C.wu[s][:], wallb[:, ou + j * 2048:ou + (j + 1) * 2048].rearrange("p (k c) -> p k c", k=KC),
               ("wu", s), [], [("wu", s)])

    def load_d(m):
        s = m % 2
        P.load(C.wd[s][:], wallb[:, od + m * 5632:od + (m + 1) * 5632].rearrange("p (j c) -> p j c", j=FC),
               ("wd", s), [], [("wd", s)])

    for j in range(NW - 1):
        load_gu(j)
    prenorm(P, C, gpre, W)
    for j in range(FC):
        if j + NW - 1 < FC:
            load_gu(j + NW - 1)
        if j == 20:
            load_d(0)
        if j == 32:
            load_d(1)
        s = j % NW
        b = j % 2
        for k in range(KC):
            P.mm(C.psA[b][:, 0:W], C.wg[s][:, k, :], C.xn[:, k, 0:W], k == 0, k == KC - 1, [("wg", s)] + XN, [("psA", b)])
        for k in range(KC):
            P.mm(C.psB[b][:, 0:W], C.wu[s][:, k, :], C.xn[:, k, 0:W], k == 0, k == KC - 1, [("wu", s)] + XN, [("psB", b)])
        P.act(C.sg[:, b, 0:W], C.psA[b][:, 0:W], AF.Silu, [("psA", b)], [("tmpc", b)])
        P.tt(C.hT[:, j, 0:W], C.psB[b][:, 0:W], C.sg[:, b, 0:W], ALU.mult, [("psB", b), ("tmpc", b)], [hreg(j)])
    pending = None
    for m in range(KC):
        s = m % 2
        b = m % 2
        for j in range(FC):
            P.mm(C.psC[b][:, 0:W], C.wd[s][:, j, :], C.hT[:, j, 0:W], j == 0, j == FC - 1, [("wd", s), hreg(j)], [("psC", b)])
        if m + 2 < KC:
            load_d(m + 2)
        pending = evac_y_chunk(P, C, C.psC[b][:, 0:W], ("psC", b), m, gpost, pending, W)
    finish_ssq(P, C, pending, W)
    postnorm_residual(P, C, True, W)


def cast_cols(P, wall, wallb, c0, c1):
    CW = 4096
    for i in range(c0 // CW, c1 // CW):
        P.add("pool", lambda e, i=i: e.dma_start(out=wallb[:, i * CW:(i + 1) * CW], in_=wall[:, i * CW:(i + 1) * CW],
                                                   max_dma_last_dim=8192),
              [], [("wallb", i)], dma=("cast", i % 8))


class WB:
    def __init__(self, nc, bounds):
        self.bounds = bounds
        self.parts = [nc.dram_tensor("wallb%d" % i, [128, bounds[i + 1] - bounds[i]], BF16, kind="Internal").ap()
                      for i in range(len(bounds) - 1)]

    def __getitem__(self, key):
        rows, cols = key
        for i in range(len(self.parts)):
            if self.bounds[i] <= cols.start and cols.stop <= self.bounds[i + 1]:
                return self.parts[i][rows, cols.start - self.bounds[i]:cols.stop - self.bounds[i]]
        raise ValueError("weight slice straddles scratch tensors: %r" % (cols,))


def dram(nc, name, shape, dt, kind):
    return nc.dram_tensor(name, list(shape), dt, kind=kind).ap()


def gla_inproj_tile(P, C, G, wallb, offs, ti, gl, need_qr):
    t0 = ti * T
    NW = C.NW
    if need_qr:
        P.store(G.x1T[:, :, t0:t0 + T].rearrange("k p t -> p k t"), C.xs[:], "st_x1", XS, [("x1T", ti)])
    oq = offs["gqkr"]
    ov = offs["gv"]
    js = list(range(32)) if need_qr else list(range(8, 16))

    def load_q(i):
        s = i % NW
        j = js[i]
        P.load(C.wg[s][:], wallb[:, oq + j * 2048:oq + (j + 1) * 2048].rearrange("p (k c) -> p k c", k=KC),
               ("wg", s), [], [("wg", s)])

    def load_v(hh):
        s = hh % 2
        P.load(C.wd[s][:, 0:32, :], wallb[:, ov + hh * 4096:ov + (hh + 1) * 4096].rearrange("p (k c) -> p k c", k=32),
               ("wd", s), [], [("wd", s)])

    for i in range(NW - 1):
        load_q(i)
    load_v(0)
    load_v(1)
    prenorm(P, C, gl * 96 + 2 * 16)
    qk = C.hT[:, 0:16, :]
    rs = C.hT[:, 16:32, :]
    last = [C.hT[:, 32 + 4 * i:36 + 4 * i, :].rearrange("p a t -> p (a t)").bitcast(F32) for i in range(2)]
    for i, j in enumerate(js):
        if i + NW - 1 < len(js):
            load_q(i + NW - 1)
        s = i % NW
        b = i % 2
        for k in range(KC):
            P.mm(C.psA[b][:], C.wg[s][:, k, :], C.xn[:, k, :], k == 0, k == KC - 1, [("wg", s)] + XN, [("psA", b)])
        if j < 8:
            P.act(qk[:, j, :], C.psA[b][:], AF.Copy, [("psA", b)], ["h0"], scale=1.0 / 16.0)
        elif j < 16:
            P.act(qk[:, j, :], C.psA[b][:], AF.Copy, [("psA", b)], ["h0"])
        else:
            P.act(rs[:, j - 16, :], C.psA[b][:], AF.Silu, [("psA", b)], ["h1"])
        if j == 15:
            if need_qr:
                P.store(G.qT[:, :, t0:t0 + T].rearrange("k p t -> p k t"), qk[:, 0:8, :], "st_q", ["h0"], [("qT", ti)])
            P.store(G.kT[:, :, t0:t0 + T].rearrange("k p t -> p k t"), qk[:, 8:16, :], "st_k", ["h0"], [("kT", ti)])
    if need_qr:
        P.store(G.rsT[:, :, t0:t0 + T].rearrange("k p t -> p k t"), rs, "st_r", ["h1"], [("rsT", ti)])
    vst = C.vst
    for hh in range(8):
        s = hh % 2
        wv = C.wd[s][:, 0:32, :].rearrange("p (k a) c -> p k (a c)", k=KC)
        for tc in range(4):
            b = (hh * 4 + tc) % 2
            for k in range(KC):
                P.mm(C.psB[b][:, 0:256], C.xn[:, k, tc * 128:(tc + 1) * 128], wv[:, k, :], k == 0, k == KC - 1,
                     [("wd", s)] + XN, [("psB", b)])
            P.copy(vst[:, tc, hh * 256:(hh + 1) * 256], C.psB[b][:, 0:256], [("psB", b)], ["U1"])
        if hh + 2 < 8:
            load_v(hh + 2)
    P.store(G.v[ti * 4:(ti + 1) * 4].rearrange("c p f -> p c f"), vst, "st_v", ["U1"], [("v", ti)])
    for dr in range(2):
        for k in range(KC):
            P.mm(C.psC[dr][0:16, :], C.wz[:, k, dr * 16:(dr + 1) * 16], C.xn[:, k, :], k == 0, k == KC - 1,
                 ["wz"] + XN, [("psC", dr)])
        P.copy(C.zaug[dr][0:16, :], C.psC[dr][0:16, :], [("psC", dr)], [("zaug", dr)])
    for dr in range(2):
        for tc in range(4):
            sl = (dr * 4 + tc) % 2
            for hf in range(2):
                b = hf
                P.mm(C.psA[b][:], C.zaug[dr][0:17, tc * 128:(tc + 1) * 128], C.wup[0:17, dr, hf * 512:(hf + 1) * 512],
                     True, True, [("zaug", dr), "wup"], [("psA", b)])
                P.act(C.tmpc[:, b, :], C.psA[b][:], AF.Exp, [("psA", b)], [("tmpc", b)], scale=-1.0)
                P.act(last[sl][:, hf * 512:(hf + 1) * 512], C.tmpc[:, b, :], AF.Ln, [("tmpc", b), "oneT"], [("h2", sl)],
                      bias=C.oneT[:], scale=1.0)
            P.store(G.la[dr, ti * 4 + tc], last[sl], ("st_la", sl), [("h2", sl)], [("la", dr, ti * 4 + tc)])


def alloc_gla_in(nc, st, C):
    C.wz = st.enter_context(nc.sbuf_tensor(_u("wz"), [128, KC, 32], BF16))
    C.zaug = [st.enter_context(nc.sbuf_tensor(_u("zaug%d" % i), [17, T], BF16)) for i in range(2)]
    C.wup = st.enter_context(nc.sbuf_tensor(_u("wup"), [17, 2, 1024], BF16))
    C.oneT = st.enter_context(nc.sbuf_tensor(_u("oneT"), [128, 1], F32))


def init_gla_in(P, C, wallb, offs):
    P.memset(C.oneT[:], 1.0, ["oneT"])
    for dr in range(2):
        P.memset(C.zaug[dr][:], 1.0, [("zaug", dr)])
    P.load(C.wup[:], wallb[0:17, offs["gwup"]:offs["gwup"] + 2048].rearrange("p (d f) -> p d f", d=2), "wupl", [], ["wup"])
    P.load(C.wz[:], wallb[:, offs["gz"]:offs["gz"] + 512].rearrange("p (k c) -> p k c", k=KC), "wz", [], ["wz"])


def alloc_scan(nc, st, S):
    S.Sf = st.enter_context(nc.sbuf_tensor(_u("Sf"), [128, 4, 2, 512], F32))
    S.Sb = st.enter_context(nc.sbuf_tensor(_u("Sb"), [128, 4, 2, 512], BF16))
    S.keep = st.enter_context(nc.sbuf_tensor(_u("keep"), [128, 8], F32))
    S.qc = [st.enter_context(nc.sbuf_tensor(_u("qc%d" % i), [128, 8, CH], BF16)) for i in range(2)]
    S.kc = [st.enter_context(nc.sbuf_tensor(_u("kc%d" % i), [128, 8, CH], BF16)) for i in range(2)]
    S.vc = [st.enter_context(nc.sbuf_tensor(_u("vc%d" % i), [128, 2048], BF16)) for i in range(2)]
    S.lac = [st.enter_context(nc.sbuf_tensor(_u("lac%d" % i), [128, 1024], F32)) for i in range(2)]
    S.ofc = [st.enter_context(nc.sbuf_tensor(_u("ofc%d" % i), [128, 16, CH], F32)) for i in range(2)]
    S.ost = [st.enter_context(nc.sbuf_tensor(_u("ost%d" % i), [128, 16, CH], F32)) for i in range(2)]
    S.E1 = st.enter_context(nc.sbuf_tensor(_u("E1"), [128, 8, CH], F32))
    S.E2 = st.enter_context(nc.sbuf_tensor(_u("E2"), [128, 8, CH], F32))
    S.qin = st.enter_context(nc.sbuf_tensor(_u("qin"), [128, 8, CH], BF16))
    S.kin = st.enter_context(nc.sbuf_tensor(_u("kin"), [128, 8, CH], BF16))
    S.kend = st.enter_context(nc.sbuf_tensor(_u("kend"), [128, 8, CH], BF16))
    S.ke = st.enter_context(nc.sbuf_tensor(_u("ke"), [128, 8, CH], BF16))
    S.sm = st.enter_context(nc.sbuf_tensor(_u("sm"), [128, 4, CH], BF16))
    S.dec = st.enter_context(nc.sbuf_tensor(_u("dec"), [128, 8], F32))
    S.tri = st.enter_context(nc.sbuf_tensor(_u("tri"), [128, 2, CH], F32))
    S.msk = st.enter_context(nc.sbuf_tensor(_u("msk"), [128, 2, 4, CH], F32))
    S.ident = st.enter_context(nc.sbuf_tensor(_u("ident"), [128, CH], BF16))
    S.identf = st.enter_context(nc.sbuf_tensor(_u("identf"), [128, CH], F32))
    S.psb = st.enter_context(nc.psum_tensor(_u("psb"), [128, 8, CH], F32))
    S.pssc = st.enter_context(nc.psum_tensor(_u("pssc"), [128, 4, CH], F32))
    S.pstr = st.enter_context(nc.psum_tensor(_u("pstr"), [128, 8, CH], BF16))
    S.pso = [st.enter_context(nc.psum_tensor(_u("pso%d" % i), [128, 4, CH], F32)) for i in range(2)]
    S.pssu = [st.enter_context(nc.psum_tensor(_u("pssu%d" % i), [128, 512], F32)) for i in range(2)]


def init_scan(P, S, consts):
    P.load(S.tri[:], consts["tri"], "c_tri", [], ["tri"])
    P.load(S.msk[:], consts["msk"], "c_msk", [], ["msk"])
    P.load(S.identf[:], consts["ident"], "c_id", [], ["identf"])
    P.load(S.keep[:], consts["keep"], "c_keep", [], ["keep"])
    P.copy(S.ident[:], S.identf[:], ["identf"], ["ident"])


def scan_dir(P, S, G, dr, seq, resets):
    lastcol = CH - 1 if dr == 0 else 0
    P.memset(S.Sf[:], 0.0, [("Sf", h) for h in range(4)])
    sb_valid = False
    n = len(seq)

    def loads(i):
        c, full = seq[i]
        sl = i % 2
        c0 = c * CH
        P.load(S.kc[sl][:], G.kT[:, :, c0:c0 + CH].rearrange("k p t -> p k t"), ("ld_k", sl), [("kT", c // 4)], [("kc", sl)])
        P.load(S.vc[sl][:], G.v[c], ("ld_v", sl), [("v", c // 4)], [("vc", sl)])
        P.load(S.lac[sl][:], G.la[dr, c], ("ld_la", sl), [("la", dr, c)], [("lac", sl)])
        if full:
            P.load(S.qc[sl][:], G.qT[:, :, c0:c0 + CH].rearrange("k p t -> p k t"), ("ld_q", sl), [("qT", c // 4)], [("qc", sl)])
            if dr == 1:
                P.load(S.ofc[sl][:], G.oT[:, :, c0:c0 + CH].rearrange("k p t -> p k t"), ("ld_of", sl), [("oT", c)], [("ofc", sl)])

    loads(0)
    for i in range(n):
        c, full = seq[i]
        sl = i % 2
        c0 = c * CH
        if i + 1 < n:
            loads(i + 1)
        if i in resets:
            col = resets[i]
            for h in range(4):
                P.ts(S.Sf[:, h], S.Sf[:, h], S.keep[:, col:col + 1], None, ALU.mult, None, [("Sf", h), "keep"], [("Sf", h)])
            sb_valid = False
        if full and not sb_valid:
            P.act(S.Sb[:], S.Sf[:], AF.Copy, [("Sf", h) for h in range(4)], [("Sb", h) for h in range(4)])
            sb_valid = True
        for fc in range(8):
            P.mm(S.psb[:, fc, :], S.lac[sl][:, fc * 128:(fc + 1) * 128], S.tri[:, dr, :], True, True,
                 [("lac", sl), "tri"], ["psb"])
        P.act(S.E2[:], S.psb[:], AF.Exp, ["psb"], ["E2"], scale=-1.0)
        P.act(S.dec[:], S.psb[:, :, lastcol], AF.Exp, ["psb"], ["dec"])
        if full:
            P.act(S.E1[:], S.psb[:], AF.Exp, ["psb"], ["E1"])
            P.tt(S.qin[:], S.qc[sl][:], S.E1[:], ALU.mult, [("qc", sl), "E1"], ["qin"])
        P.tt(S.kin[:], S.kc[sl][:], S.E2[:], ALU.mult, [("kc", sl), "E2"], ["kin"])
        for fc in range(8):
            P.ts(S.kend[:, fc, :], S.kin[:, fc, :], S.dec[:, fc:fc + 1], None, ALU.mult, None, ["kin", "dec"], ["kend"])
        for fc in range(8):
            P.tr(S.pstr[:, fc, :], S.kend[:, fc, :], S.ident[:], ["kend", "ident"], ["pstr"])
        P.copy(S.ke[:], S.pstr[:], ["pstr"], ["ke"])
        if full:
            for h in range(4):
                for dc in range(2):
                    f = 2 * h + dc
                    P.mm(S.pssc[:, h, :], S.kin[:, f, :], S.qin[:, f, :], dc == 0, dc == 1, ["kin", "qin"], ["pssc"])
            P.tt(S.sm[:], S.pssc[:], S.msk[:, dr], ALU.mult, ["pssc", "msk"], ["sm"])
        for h in range(4):
            if full:
                ob = h % 2
                for j in range(4):
                    e0 = h * 512 + j * 128
                    P.mm(S.pso[ob][:, j, :], S.vc[sl][:, e0:e0 + 128], S.sm[:, h, :], True, False,
                         [("vc", sl), "sm"], [("pso", ob)])
                    for dc in range(2):
                        P.mm(S.pso[ob][:, j, :], S.Sb[:, h, dc, j * 128:(j + 1) * 128], S.qin[:, 2 * h + dc, :], False, dc == 1,
                             [("Sb", h), "qin"], [("pso", ob)])
                if dr == 0:
                    P.act(S.ost[sl][:, 4 * h:4 * h + 4, :], S.pso[ob][:], AF.Copy, [("pso", ob)], [("ost", sl)])
                else:
                    P.tt(S.ost[sl][:, 4 * h:4 * h + 4, :], S.pso[ob][:], S.ofc[sl][:, 4 * h:4 * h + 4, :], ALU.add,
                         [("pso", ob), ("ofc", sl)], [("ost", sl)])
            for dc in range(2):
                P.mm(S.pssu[dc][:], S.ke[:, 2 * h + dc, :], S.vc[sl][:, h * 512:(h + 1) * 512], True, True,
                     ["ke", ("vc", sl)], [("pssu", dc)])
            for dc in range(2):
                P.stt(S.Sf[:, h, dc, :], S.Sf[:, h, dc, :], S.dec[:, 2 * h + dc:2 * h + dc + 1], S.pssu[dc][:],
                      ALU.mult, ALU.add, [("Sf", h), "dec", ("pssu", dc)], [("Sf", h)])
            if full:
                P.act(S.Sb[:, h], S.Sf[:, h], AF.Copy, [("Sf", h)], [("Sb", h)])
        if not full:
            sb_valid = False
        if full:
            P.store(G.oT[:, :, c0:c0 + CH].rearrange("k p t -> p k t"), S.ost[sl][:], ("st_o", sl), [("ost", sl)], [("oT", c)])


def gla_out_tile(P, C, G, wallb, offs, pieces, W=T):
    NW = C.NW
    oo = offs["go"]

    def load_o(m):
        s = m % NW
        P.load(C.wg[s][:], wallb[:, oo + m * 2048:oo + (m + 1) * 2048].rearrange("p (k c) -> p k c", k=KC),
               ("wg", s), [], [("wg", s)])

    for m in range(NW - 1):
        load_o(m)
    rs = C.hT[:, 16:32, :]
    for (g0, c0, n) in pieces:
        chs = sorted(set(range(g0 // CH, (g0 + n - 1) // CH + 1)))
        P.load(C.xs[:, :, c0:c0 + n], G.x1T[:, :, g0:g0 + n].rearrange("k p t -> p k t"), "ld_xs", [("x1T", g0 // T)], XS)
        P.load(C.y[:, :, c0:c0 + n], G.oT[:, :, g0:g0 + n].rearrange("k p t -> p k t"), "ld_o", [("oT", c) for c in chs], YY)
        P.load(rs[:, :, c0:c0 + n], G.rsT[:, :, g0:g0 + n].rearrange("k p t -> p k t"), "ld_rs", [("rsT", g0 // T)], ["h1"])
    for c in range(KC):
        P.act(C.sq[:, c, 0:W], C.y[:, c, 0:W], AF.Square, YY, [("sq", c)])
    for h in range(4):
        for j in range(4):
            P.mm(C.pss[:, 0:W], C.ones[:], C.sq[:, 4 * h + j, 0:W], j == 0, j == 3, [("sq", 4 * h + j), "ones"], ["pss"])
        rstd_from_pss(P, C, 512, False, W)
        for j in range(4):
            c = 4 * h + j
            b = c % 2
            P.stt(C.tmpc[:, b, 0:W], C.y[:, c, 0:W], C.hg[:, j:j + 1], C.rstd[:, 0:W], ALU.mult, ALU.mult,
                  YY + ["hg", "rstd"], [("tmpc", b)])
            P.tt(rs[:, c, 0:W], C.tmpc[:, b, 0:W], rs[:, c, 0:W], ALU.mult, [("tmpc", b), "h1"], ["h1"])
    pending = None
    for m in range(KC):
        if m + NW - 1 < KC:
            load_o(m + NW - 1)
        s = m % NW
        b = m % 2
        for k in range(KC):
            P.mm(C.psC[b][:, 0:W], C.wg[s][:, k, :], rs[:, k, 0:W], k == 0, k == KC - 1, [("wg", s), "h1"], [("psC", b)])
        pending = evac_y_chunk(P, C, C.psC[b][:, 0:W], ("psC", b), m, 0 * 96 + 3 * 16, pending, W)
    finish_ssq(P, C, pending, W)
    postnorm_residual(P, C, False, W)


def pool_tile(P, C, G, wallb, offs, ti):
    t0 = ti * T
    P.load(C.xs[:], G.x4e[:, :, t0:t0 + T].rearrange("k p t -> p k t"), "ld_xs", [("x4e", ti)], XS)
    P.load(C.hp, G.h1e[:, :, 56 + t0:56 + t0 + T + 16].rearrange("k p t -> p k t"), "ld_hp",
           [("h1e", "L"), ("h1e", "R"), ("h1e", ti - 1), ("h1e", ti), ("h1e", ti + 1)], YY)
    icnt = C.wd[1][:, 0:32, :].rearrange("p a c -> p (a c)").bitcast(F32).rearrange("p (g t) -> p g t", g=4)
    P.load(icnt.rearrange("p (o g) t -> p o (g t)", o=1), G.icnt[ti].partition_broadcast(128), ("wd", 1), [], [("wd", 1)])
    wslots = [C.wg[0], C.wg[1], C.wu[0], C.wu[1]]
    wnames = [("wg", 0), ("wg", 1), ("wu", 0), ("wu", 1)]
    for q in range(4):
        P.load(wslots[q][:], wallb[:, offs["pw"] + q * 2048:offs["pw"] + (q + 1) * 2048].rearrange("p (k c) -> p k c", k=KC),
               wnames[q], [], [wnames[q]])
    hp = C.hp
    pflat = C.hT[:, 16:44, :].rearrange("p a t -> p (a t)").bitcast(F32)
    PW = T + 16
    pa = pflat[:, 0:4 * PW].rearrange("p (c t) -> p c t", c=4)
    pb = pflat[:, 4 * PW:8 * PW].rearrange("p (c t) -> p c t", c=4)
    for g in range(4):
        cs = slice(4 * g, 4 * g + 4)
        if g == 0:
            P.tt(pa[:, :, 0:T], hp[:, cs, 7:7 + T], hp[:, cs, 8:8 + T], ALU.add, YY, HR)
            ssum = pa
        else:
            P.tt(pa[:, :, 0:T + 15], hp[:, cs, 0:T + 15], hp[:, cs, 1:T + 16], ALU.add, YY, HR)
            if g == 1:
                P.tt(pb[:, :, 0:T], pa[:, :, 6:6 + T], pa[:, :, 8:8 + T], ALU.add, HR, HR)
                ssum = pb
            else:
                P.tt(pb[:, :, 0:T + 13], pa[:, :, 0:T + 13], pa[:, :, 2:T + 15], ALU.add, HR, HR)
                if g == 2:
                    P.tt(pa[:, :, 0:T], pb[:, :, 4:4 + T], pb[:, :, 8:8 + T], ALU.add, HR, HR)
                    ssum = pa
                else:
                    P.tt(pa[:, :, 0:T + 9], pb[:, :, 0:T + 9], pb[:, :, 4:T + 13], ALU.add, HR, HR)
                    P.tt(pb[:, :, 0:T], pa[:, :, 0:T], pa[:, :, 8:8 + T], ALU.add, HR, HR)
                    ssum = pb
        for cc in range(4):
            c = 4 * g + cc
            b = c % 2
            P.tt(C.tmpc[:, b, :], ssum[:, cc, 0:T], icnt[:, g, :], ALU.mult, HR + [("wd", 1)], [("tmpc", b)])
            P.tt(C.hT[:, c, :], C.tmpc[:, b, :], hp[:, c, 8:8 + T], ALU.subtract, [("tmpc", b)] + YY, ["h0"])
    pending = None
    g3 = 1 * 96 + 3 * 16
    for oc in range(KC):
        g = oc // 4
        b = oc % 2
        q = oc // 4
        for cc in range(4):
            P.mm(C.psC[b][:], wslots[q][:, (oc % 4) * 4 + cc, :], C.hT[:, 4 * g + cc, :], cc == 0, cc == 3,
                 [wnames[q], "h0"], [("psC", b)])
        P.ts(C.sg[:, b, :], C.psC[b][:], C.pbc[:, oc:oc + 1], C.psc[:, oc:oc + 1], ALU.add, ALU.mult,
             [("psC", b), "pbc", "psc"], [("tmpc", b)])
        P.act(C.y[:, oc, :], C.sg[:, b, :], AF.Copy, [("tmpc", b), "gc"], YY, scale=C.gc[:, g3 + oc:g3 + oc + 1])
        P.act(C.sqc[:, b, :], C.sg[:, b, :], AF.Square, [("tmpc", b)], [("sqc", b)])
        if pending is not None:
            pm = pending
            P.mm(C.pss[:], C.ones[:], C.sqc[:, pm % 2, :], pm == 0, False, [("sqc", pm % 2), "ones"], ["pss"])
        pending = oc
    finish_ssq(P, C, pending)
    postnorm_residual(P, C, False)


def build_fused(offs, ncols, first_cols):
    nc = bass.Bass("TRN2", target_bir_lowering=False)
    EI, EO, IN = "ExternalInput", "ExternalOutput", "Internal"
    wall = dram(nc, "wall", [128, ncols], F32, EI)
    wallb = WB(nc, [0, first_cols, offs["pw"], ncols])
    gcols = dram(nc, "gcols", [128, 192], F32, EI)
    xT = dram(nc, "xT", [KC, 128, LS], F32, EI)
    consts = {"tri": dram(nc, "tri", [128, 2, CH], F32, EI), "msk": dram(nc, "msk", [128, 2, 4, CH], F32, EI),
              "ident": dram(nc, "ident", [128, CH], F32, EI), "keep": dram(nc, "keep", [128, 8], F32, EI)}
    hgain = dram(nc, "hgain", [128, 4], F32, EI)
    hmask = dram(nc, "hmask", [128, 2], F32, EI)
    pbc = dram(nc, "pbc", [128, 16], F32, EI)
    psc = dram(nc, "psc", [128, 16], F32, EI)
    outT = dram(nc, "outT", [KC, 128, NT], F32, EO)
    G = Ctx()
    G.icnt = dram(nc, "icnt", [NTILE, 1, 4 * T], F32, EI)
    G.x1T = dram(nc, "x1T", [KC, 128, LS], F32, IN)
    G.qT = dram(nc, "qT", [8, 128, LS], BF16, IN)
    G.kT = dram(nc, "kT", [8, 128, LS], BF16, IN)
    G.rsT = dram(nc, "rsT", [KC, 128, LS], BF16, IN)
    G.v = dram(nc, "v", [NCS, 128, 2048], BF16, IN)
    G.la = dram(nc, "la", [2, NCS, 128, 1024], F32, IN)
    G.oT = dram(nc, "oT", [KC, 128, LS], F32, IN)
    G.x4e = dram(nc, "x4e", [KC, 128, NT], F32, IN)
    G.h1e = dram(nc, "h1e", [KC, 128, NT + 128], F32, IN)

    own_tiles = list(range(NTS - NTILE, NTS))
    qr_tiles = set(own_tiles) | {NTS - NTILE - 1, 0}
    full_chunks = set(range(3 * (NCS // 4), NCS)) | {3 * (NCS // 4) - 1, 0}
    HW = 128
    HP = HW // 2
    halo_pieces = [(3 * NT - HP, 0, HP), (0, HP, HP)]

    with ExitStack() as gstack:
        P = Prog(nc, gstack)
        cast_cols(P, wall, wallb, 0, first_cols)
        P.emit()

        with ExitStack() as st:
            C = Ctx()
            alloc_common(nc, st, C)
            alloc_ffn_bufs(nc, st, C)
            alloc_gla_in(nc, st, C)
            cast_cols(P, wall, wallb, first_cols, ncols)
            init_common(P, C, gcols)
            init_gla_in(P, C, wallb, offs)
            for ti in range(NTS):
                t0 = ti * T
                P.load(C.xs[:], xT[:, :, t0:t0 + T].rearrange("k p t -> p k t"), "ld_xs", [], XS)
                ffn(P, C, wallb, offs["f00g"], offs["f00u"], offs["f00d"], 0 * 96 + 0 * 16, 0 * 96 + 1 * 16)
                gla_inproj_tile(P, C, G, wallb, offs, ti, 0, ti in qr_tiles)
            P.emit()

        with ExitStack() as st:
            S = Ctx()
            alloc_scan(nc, st, S)
            init_scan(P, S, consts)
            SEGC = NCS // 4
            seq = [(c, c in full_chunks and c >= SEGC) for c in range(NCS)] + [(0, True)]
            resets = {SEGC: 0, 2 * SEGC: 1, 3 * SEGC: 2, NCS: 3}
            scan_dir(P, S, G, 0, seq, resets)
            seq = []
            for sg in (2, 1, 0):
                seq += [(c, sg == 0 and c in full_chunks) for c in range((sg + 1) * SEGC - 1, sg * SEGC - 1, -1)]
            seq += [(c, True) for c in range(NCS - 1, 3 * SEGC - 1, -1)]
            seq += [(3 * SEGC - 1, True)]
            resets = {SEGC: 4, 2 * SEGC: 5, 3 * SEGC: 6, NCS: 7}
            scan_dir(P, S, G, 1, seq, resets)
            P.emit()

        with ExitStack() as st:
            C = Ctx()
            alloc_common(nc, st, C)
            alloc_ffn_bufs(nc, st, C)
            C.hg = st.enter_context(nc.sbuf_tensor(_u("hg"), [128, 4], F32))
            C.hm = st.enter_context(nc.sbuf_tensor(_u("hm"), [128, 2], F32))
            init_common(P, C, gcols)
            P.load(C.hg[:], hgain, "ld_hg", [], ["hg"])
            P.load(C.hm[:], hmask, "ld_hm", [], ["hm"])
            f01 = (offs["f01g"], offs["f01u"], offs["f01d"])
            f10 = (offs["f10g"], offs["f10u"], offs["f10d"])
            gp = 1 * 96 + 2 * 16

            def post_mixer(pieces, W):
                gla_out_tile(P, C, G, wallb, offs, pieces, W)
                ffn(P, C, wallb, f01[0], f01[1], f01[2], 0 * 96 + 4 * 16, 0 * 96 + 5 * 16, W)
                ffn(P, C, wallb, f10[0], f10[1], f10[2], 1 * 96 + 0 * 16, 1 * 96 + 1 * 16, W)

            def h1_into_y(W):
                ssq_xs(P, C, W)
                rstd_from_pss(P, C, D, False, W)
                for k in range(KC):
                    P.stt(C.y[:, k, 0:W], C.xs[:, k, 0:W], C.gc[:, gp + k:gp + k + 1], C.rstd[:, 0:W], ALU.mult, ALU.mult,
                          [("xs", k), "gc", "rstd"], YY)

            post_mixer(halo_pieces, HW)
            h1_into_y(HW)
            P.ts(C.y[:, :, 0:HP], C.y[:, :, 0:HP], C.hm[:, 0:1], None, ALU.mult, None, YY + ["hm"], YY)
            P.ts(C.y[:, :, HP:HW], C.y[:, :, HP:HW], C.hm[:, 1:2], None, ALU.mult, None, YY + ["hm"], YY)
            P.store(G.h1e[:, :, 0:HP].rearrange("k p t -> p k t"), C.y[:, :, 0:HP], "st_h1", YY, [("h1e", "L")])
            P.store(G.h1e[:, :, NT + HP:NT + HW].rearrange("k p t -> p k t"), C.y[:, :, HP:HW], "st_h1", YY, [("h1e", "R")])
            for e, ti in enumerate(own_tiles):
                e0 = e * T
                post_mixer([(ti * T, 0, T)], T)
                P.store(G.x4e[:, :, e0:e0 + T].rearrange("k p t -> p k t"), C.xs[:], "st_x4", XS, [("x4e", e)])
                h1_into_y(T)
                P.store(G.h1e[:, :, 64 + e0:64 + e0 + T].rearrange("k p t -> p k t"), C.y, "st_h1", YY, [("h1e", e)])
            P.emit()

        with ExitStack() as st:
            C = Ctx()
            alloc_common(nc, st, C)
            alloc_ffn_bufs(nc, st, C)
            C.pbc = st.enter_context(nc.sbuf_tensor(_u("pbc"), [128, 16], F32))
            C.psc = st.enter_context(nc.sbuf_tensor(_u("psc"), [128, 16], F32))
            init_common(P, C, gcols)
            P.load(C.pbc[:], pbc, "ld_pb", [], ["pbc"])
            P.load(C.psc[:], psc, "ld_ps", [], ["psc"])
            for ti in range(NTILE):
                t0 = ti * T
                pool_tile(P, C, G, wallb, offs, ti)
                ffn(P, C, wallb, offs["f11g"], offs["f11u"], offs["f11d"], 1 * 96 + 4 * 16, 1 * 96 + 5 * 16)
                P.store(outT[:, :, t0:t0 + T].rearrange("k p t -> p k t"), C.xs[:], "st_out", XS, [("outT", ti)])
            P.emit()
        nsem = P.nsem
    return nc, nsem


def _cols(vec, n=16):
    return np.ascontiguousarray(np.asarray(vec, np.float32).reshape(n, 128).T)


def _consts():
    s = np.arange(CH)[:, None]
    c = np.arange(CH)[None, :]
    tri = np.zeros((128, 2, CH), np.float32)
    tri[:, 0, :] = np.where(s <= c, -1.0 / 16.0, 0.0)
    tri[:, 1, :] = np.where(s >= c, -1.0 / 16.0, 0.0)
    msk = np.zeros((128, 2, 4, CH), np.float32)
    msk[:, 0] = np.where(c >= s, 1.0, 0.0)[:, None, :]
    msk[:, 1] = np.where(c <= s, 1.0, 0.0)[:, None, :]
    ident = np.eye(128, dtype=np.float32)
    return tri, msk, ident


def kernel(x, ffn_w_gate, ffn_w_up, ffn_w_down, norm_gain, gla_w_in, gla_w_gate_up, gla_b_gate,
           gla_head_gain, gla_w_out, pool_w, pool_b, pool_scale):
    x = np.asarray(x, np.float32)
    B, L, _ = x.shape
    cores = list(range(NCORES))
    gcols = np.concatenate([_cols(norm_gain[li, n]) for li in range(2) for n in range(6)], axis=1)
    tri, msk, ident = _consts()

    pk = Pack()
    pack_ffn(pk, "f00", ffn_w_gate[0, 0], ffn_w_up[0, 0], ffn_w_down[0, 0])
    pack_gla_in(pk, gla_w_in[0], gla_w_gate_up[0], gla_b_gate[0])
    first_cols = pk.n + ((-pk.n) % 4096)
    if first_cols != pk.n:
        pk.add("pad0", np.zeros((128, first_cols - pk.n), np.float32))
    pk.add("go", _lhs_layout(gla_w_out[0], KC))
    pack_ffn(pk, "f01", ffn_w_gate[0, 1], ffn_w_up[0, 1], ffn_w_down[0, 1])
    pack_ffn(pk, "f10", ffn_w_gate[1, 0], ffn_w_up[1, 0], ffn_w_down[1, 0])
    pw = pool_w[0]
    wp = pw.reshape(4, 4, 128, 4, 128)
    wp = np.ascontiguousarray(wp.transpose(2, 0, 3, 1, 4)).reshape(128, -1)
    pk.add("pw", wp)
    pack_ffn(pk, "f11", ffn_w_gate[1, 1], ffn_w_up[1, 1], ffn_w_down[1, 1])
    wall = pk.build()
    nc, _ = build_fused(pk.offs, pk.n, first_cols)

    hgain = _cols(gla_head_gain[0], 4)
    pbc = _cols(pool_b[0])
    pscl = _cols(pool_scale[0])
    tpos = np.arange(NT)
    xTb = [np.ascontiguousarray(x[b].T) for b in range(B)]
    in_maps = []
    for c in cores:
        b, s = divmod(c, 4)
        order = [(s + 1 + i) % 4 for i in range(4)]
        xT = np.concatenate([xTb[b][:, g * NT:(g + 1) * NT] for g in order], axis=1).reshape(KC, 128, LS)
        keep = np.ones((128, 8), np.float32)
        for i in (1, 2, 3):
            keep[:, i - 1] = 0.0 if order[i] == 0 else 1.0
        keep[:, 3] = 0.0 if order[0] == 0 else 1.0
        prev = [order[2], order[1], order[0], order[3]]
        for i in range(4):
            keep[:, 4 + i] = 0.0 if prev[i] == 0 else 1.0
        hmask = np.ones((128, 2), np.float32)
        hmask[:, 0] = 0.0 if s == 0 else 1.0
        hmask[:, 1] = 0.0 if s == 3 else 1.0
        icnt = np.zeros((4, NT), np.float32)
        tg = tpos + s * NT
        for g, w in enumerate((2, 4, 8, 16)):
            lo = np.clip(tg - w // 2, 0, L)
            hi = np.clip(tg + w // 2, 0, L)
            icnt[g] = 1.0 / (hi - lo).astype(np.float32)
        icnt = np.ascontiguousarray(icnt.reshape(4, NTILE, T).transpose(1, 0, 2)).reshape(NTILE, 1, 4 * T)
        in_maps.append({"wall": wall, "gcols": gcols, "xT": xT, "tri": tri, "msk": msk, "ident": ident, "keep": keep,
                        "hgain": hgain, "hmask": hmask, "pbc": pbc, "psc": pscl, "icnt": icnt})
    res = run_bass_kernel_spmd(nc, in_maps, core_ids=cores).results
    out = np.empty((B, L, D), np.float32)
    for c in cores:
        b, s = divmod(c, 4)
        out[b, s * NT:(s + 1) * NT, :] = res[c]["outT"].reshape(D, NT).T
    return out
```
